# Optimizing a Trainium2 kernel written in Bass

```python
import math
import jax, jax.numpy as jnp
from jax import lax
import numpy as np

D_MODEL = 1024
BATCH = 8
SEQ = 2048
DEPTH = 2

HEAD_DIM = 64
MEM_HEADS = 4
MEM_WIDTH = MEM_HEADS * HEAD_DIM
SB_HEADS = (D_MODEL - MEM_WIDTH) // (2 * HEAD_DIM)
MOBA_HEADS = SB_HEADS
SB_WIDTH = SB_HEADS * HEAD_DIM
MOBA_WIDTH = MOBA_HEADS * HEAD_DIM
MIX_WIDTH = SB_WIDTH + MOBA_WIDTH + MEM_WIDTH
IN_COLS = 4 * SB_WIDTH + 4 * MOBA_WIDTH + 2 * MEM_WIDTH
MEM_LEN = 256
SB_BLOCK = 128
MOBA_BLOCK = 256
MOBA_TOPK = 3
MOBA_Q_CHUNK = 64
ROPE_THETA = 500000.0
ROPE_DIMS = HEAD_DIM // 4
NORM_EPS = 1e-6
NEG_INF = -1e30

kernel_name = "hybrid_stickbreak_moba_memxattn"


def rms_norm(x, g):
    xf = x.astype(jnp.float32)
    y = xf * lax.rsqrt(jnp.mean(xf * xf, axis=-1, keepdims=True) + NORM_EPS)
    return (y * g.astype(jnp.float32)).astype(x.dtype)


def split_heads(t, n_heads):
    b, s, _ = t.shape
    return t.reshape(b, s, n_heads, HEAD_DIM).transpose(0, 2, 1, 3)


def merge_heads(t):
    b, h, s, d = t.shape
    return t.transpose(0, 2, 1, 3).reshape(b, s, h * d)


def partial_rotary(t, pos):
    half = ROPE_DIMS // 2
    inv_freq = jnp.float32(ROPE_THETA) ** (-jnp.arange(half, dtype=jnp.float32) * 2.0 / ROPE_DIMS)
    ang = pos.astype(jnp.float32)[:, None] * inv_freq[None, :]
    cos, sin = jnp.cos(ang), jnp.sin(ang)
    tf = t.astype(jnp.float32)
    x1, x2, rest = tf[..., :half], tf[..., half:ROPE_DIMS], tf[..., ROPE_DIMS:]
    out = jnp.concatenate([x1 * cos - x2 * sin, x2 * cos + x1 * sin, rest], axis=-1)
    return out.astype(t.dtype)


def stick_breaking_attention(q, k, v):
    T = q.shape[2]
    scale = HEAD_DIM ** -0.5
    outs = []
    for i in range(T // SB_BLOCK):
        q0, q1 = i * SB_BLOCK, (i + 1) * SB_BLOCK
        qb = q[:, :, q0:q1]
        kb, vb = k[:, :, :q1], v[:, :, :q1]
        z = jnp.einsum('bhqd,bhkd->bhqk', qb, kb, preferred_element_type=jnp.float32) * scale
        past = jnp.arange(q1)[None, :] < jnp.arange(q0, q1)[:, None]
        log_beta = jax.nn.log_sigmoid(z)
        log_one_minus = jnp.where(past, log_beta - z, 0.0)
        later = lax.cumsum(log_one_minus, axis=3, reverse=True) - log_one_minus
        w = jnp.where(past, jnp.exp(log_beta + later), 0.0)
        outs.append(jnp.einsum('bhqk,bhkd->bhqd', w.astype(v.dtype), vb))
    return jnp.concatenate(outs, axis=2)


def moba_attention(q, k, v):
    B, H, T, dh = q.shape
    n_blk = -(-T // MOBA_BLOCK)
    Tp = n_blk * MOBA_BLOCK
    pad = [(0, 0), (0, 0), (0, Tp - T), (0, 0)]
    kp, vp = jnp.pad(k, pad), jnp.pad(v, pad)
    kblk = kp.reshape(B, H, n_blk, MOBA_BLOCK, dh)
    vblk = vp.reshape(B, H, n_blk, MOBA_BLOCK, dh)
    kmean = jnp.mean(kblk.astype(jnp.float32), axis=3)
    topk = min(MOBA_TOPK, n_blk - 1)
    scale = dh ** -0.5
    bi = jnp.arange(B)[:, None, None, None]
    hi = jnp.arange(H)[None, :, None, None]

    def chunk(ci):
        c0 = ci * MOBA_Q_CHUNK
        qc = lax.dynamic_slice_in_dim(q, c0, MOBA_Q_CHUNK, axis=2)
        t_pos = c0 + jnp.arange(MOBA_Q_CHUNK)
        own = c0 // MOBA_BLOCK
        own_start = own * MOBA_BLOCK
        k_own = lax.dynamic_slice_in_dim(kp, own_start, MOBA_BLOCK, axis=2)
        v_own = lax.dynamic_slice_in_dim(vp, own_start, MOBA_BLOCK, axis=2)
        s_own = jnp.einsum('bhqd,bhkd->bhqk', qc, k_own, preferred_element_type=jnp.float32) * scale
        causal = (own_start + jnp.arange(MOBA_BLOCK))[None, :] <= t_pos[:, None]
        s_own = jnp.where(causal, s_own, NEG_INF)
        if topk > 0:
            gate = jnp.einsum('bhqd,bhnd->bhqn', qc.astype(jnp.float32), kmean)
            gate = jnp.where(jnp.arange(n_blk) < own, gate, -jnp.inf)
            _, sel = lax.top_k(gate, topk)
            sel_valid = sel < own
            k_sel = kblk[bi, hi, sel]
            v_sel = vblk[bi, hi, sel]
            s_sel = jnp.einsum('bhqd,bhqnkd->bhqnk', qc, k_sel, preferred_element_type=jnp.float32) * scale
            s_sel = jnp.where(sel_valid[..., None], s_sel, NEG_INF)
            s_sel = s_sel.reshape(B, H, MOBA_Q_CHUNK, topk * MOBA_BLOCK)
            p = jax.nn.softmax(jnp.concatenate([s_sel, s_own], axis=-1), axis=-1)
            p_sel = p[..., :topk * MOBA_BLOCK].reshape(B, H, MOBA_Q_CHUNK, topk, MOBA_BLOCK)
            p_own = p[..., topk * MOBA_BLOCK:]
            o = (jnp.einsum('bhqnk,bhqnkd->bhqd', p_sel.astype(v.dtype), v_sel)
                 + jnp.einsum('bhqk,bhkd->bhqd', p_own.astype(v.dtype), v_own))
        else:
            p_own = jax.nn.softmax(s_own, axis=-1)
            o = jnp.einsum('bhqk,bhkd->bhqd', p_own.astype(v.dtype), v_own)
        return o

    outs = lax.map(chunk, jnp.arange(T // MOBA_Q_CHUNK))
    return outs.transpose(1, 2, 0, 3, 4).reshape(B, H, T, dh)


def memory_attention(q, mk, mv):
    s = jnp.einsum('bhqd,bhmd->bhqm', q, mk, preferred_element_type=jnp.float32) * (HEAD_DIM ** -0.5)
    p = jax.nn.softmax(s, axis=-1)
    return jnp.einsum('bhqm,bhmd->bhqd', p.astype(mv.dtype), mv)


def setup_inputs(seed: int = 0) -> dict:
    key = jax.random.key(seed)
    ks = jax.random.split(key, 8)
    x = jax.random.normal(ks[0], (BATCH, SEQ, D_MODEL), jnp.float32)
    mem = jax.random.normal(ks[1], (BATCH, MEM_LEN, D_MODEL), jnp.float32)
    norm_g = 1.0 + 0.02 * jax.random.normal(ks[2], (DEPTH, D_MODEL), jnp.float32)
    w_in = jax.random.normal(ks[3], (DEPTH, D_MODEL, IN_COLS), jnp.float32) * D_MODEL ** -0.5
    mem_norm_g = 1.0 + 0.02 * jax.random.normal(ks[4], (DEPTH, D_MODEL), jnp.float32)
    w_mem_kv = jax.random.normal(ks[5], (DEPTH, D_MODEL, 2 * MEM_WIDTH), jnp.float32) * D_MODEL ** -0.5
    w_out = jax.random.normal(ks[6], (DEPTH, MIX_WIDTH, D_MODEL), jnp.float32) * MIX_WIDTH ** -0.5
    final_norm_g = 1.0 + 0.02 * jax.random.normal(ks[7], (D_MODEL,), jnp.float32)
    return {"x": x, "mem": mem, "norm_g": norm_g, "w_in": w_in, "mem_norm_g": mem_norm_g,
            "w_mem_kv": w_mem_kv, "w_out": w_out, "final_norm_g": final_norm_g}


def reference(x, mem, norm_g, w_in, mem_norm_g, w_mem_kv, w_out, final_norm_g):
    T = x.shape[1]
    pos = jnp.arange(T)
    widths = [SB_WIDTH] * 4 + [MOBA_WIDTH] * 4 + [MEM_WIDTH] * 2
    split_at = np.cumsum(widths)[:-1].tolist()
    for layer in range(DEPTH):
        h = rms_norm(x, norm_g[layer])
        proj = jnp.einsum('btd,dc->btc', h, w_in[layer])
        sb_q, sb_k, sb_v, sb_g, mb_q, mb_k, mb_v, mb_g, mem_q, mem_g = jnp.split(proj, split_at, axis=-1)

        o_sb = stick_breaking_attention(split_heads(sb_q, SB_HEADS), split_heads(sb_k, SB_HEADS),
                                        split_heads(sb_v, SB_HEADS))
        o_sb = merge_heads(o_sb) * jax.nn.silu(sb_g)

        q_mb = partial_rotary(split_heads(mb_q, MOBA_HEADS), pos)
        k_mb = partial_rotary(split_heads(mb_k, MOBA_HEADS), pos)
        o_mb = moba_attention(q_mb, k_mb, split_heads(mb_v, MOBA_HEADS))
        o_mb = merge_heads(o_mb) * jax.nn.silu(mb_g)

        m = rms_norm(mem, mem_norm_g[layer])
        mkv = jnp.einsum('bmd,dc->bmc', m, w_mem_kv[layer])
        mk, mv = jnp.split(mkv, 2, axis=-1)
        o_mem = memory_attention(split_heads(mem_q, MEM_HEADS), split_heads(mk, MEM_HEADS),
                                 split_heads(mv, MEM_HEADS))
        o_mem = merge_heads(o_mem) * jax.nn.silu(mem_g)

        mixed = jnp.concatenate([o_sb, o_mb, o_mem], axis=-1)
        x = x + jnp.einsum('btc,cd->btd', mixed, w_out[layer])
    return rms_norm(x, final_norm_g)
```

```python
import os
import numpy as np
import ml_dtypes
from contextlib import ExitStack

import concourse.bass as bass
import concourse.mybir as mybir
from concourse.bass_utils import run_bass_kernel_spmd

F32 = mybir.dt.float32
BF16 = mybir.dt.bfloat16
AF = mybir.ActivationFunctionType
ALU = mybir.AluOpType
AX = mybir.AxisListType

T = 2048
D = 1024
L = 2
NT = 16
MEM = 256
NEG = -30000.0
EPS = 1e-6
STRICT = os.environ.get("KSTRICT", "1") == "1"
SIDE_MID = os.environ.get("KSIDEMID", "1") == "1"


class Prog:
    ENG = ("pe", "act", "dve", "pool", "sp")

    def __init__(self, nc):
        self.nc = nc
        self.ins = {e: [] for e in self.ENG}
        self.res = {}
        self.chan_n = {}

    def _deps(self, me, eng, reads, writes):
        deps = set()
        raw = set()
        rr = set()
        for r in reads:
            st = self.res.setdefault(r, [None, []])
            if st[0] is not None:
                deps.add(st[0])
                raw.add(st[0])
            if isinstance(r, tuple) and r[0] in ("ps", "psb"):
                for rd in st[1]:
                    if not (rd[0] == "e" and rd[1] == eng):
                        rr.add(rd)
        for w in writes:
            st = self.res.setdefault(w, [None, []])
            if st[0] is not None:
                deps.add(st[0])
            for rd in st[1]:
                deps.add(rd)
        for r in reads:
            self.res[r][1].append(me)
        for w in writes:
            self.res[w] = [me, []]
        out = []
        for d in deps | rr:
            if d == me:
                continue
            if d[0] == "e" and d[1] == eng:
                if eng == "pe" or (d not in raw and not STRICT):
                    continue
            out.append(d)
        return out

    def alias_transfer(self, from_list, to_list):
        acc = []
        for r in from_list:
            st = self.res.get(r)
            if st is None:
                continue
            if st[0] is not None:
                acc.append(st[0])
            acc.extend(st[1])
        for r in to_list:
            st = self.res.setdefault(r, [None, []])
            st[1] = list(st[1]) + acc

    def op(self, eng, meth, *args, reads=(), writes=(), **kw):
        idx = len(self.ins[eng])
        me = ("e", eng, idx)
        deps = self._deps(me, eng, reads, writes)
        self.ins[eng].append(dict(meth=meth, args=args, kw=kw, deps=deps, sig=False, chan=None))
        return me

    def dma(self, chan, out, in_, reads=(), writes=(), eng="sp", **kw):
        k = self.chan_n.get(chan, 0) + 1
        self.chan_n[chan] = k
        me = ("d", chan, k)
        deps = self._deps(me, eng, reads, writes)
        self.ins[eng].append(dict(meth="dma_start", args=(), kw=dict(out=out, in_=in_, **kw), deps=deps,
                                  sig=False, chan=chan))
        return me

    def wait_all_dma(self, chan, eng="sp"):
        k = self.chan_n[chan]
        self.ins[eng].append(dict(meth=None, args=(), kw={}, deps=[("d", chan, k)], sig=False, chan=None))

    def emit(self):
        nc = self.nc
        for e in self.ENG:
            for ins in self.ins[e]:
                for d in ins["deps"]:
                    if d[0] == "e":
                        self.ins[d[1]][d[2]]["sig"] = True
        cnt = {}
        for e in self.ENG:
            c = 0
            arr = []
            for ins in self.ins[e]:
                if ins["sig"]:
                    c += 1
                arr.append(c)
            cnt[e] = arr
        with ExitStack() as es:
            sem_e = {e: es.enter_context(nc.semaphore("sem_" + e)) for e in self.ENG}
            sem_c = {c: es.enter_context(nc.semaphore("dma_" + str(c))) for c in self.chan_n}
            block = es.enter_context(nc.Block())
            stats = {}

            def replay(e, engobj):
                waited = {}
                nw = 0
                for ins in self.ins[e]:
                    need = {}
                    for d in ins["deps"]:
                        if d[0] == "e":
                            key = ("e", d[1])
                            val = cnt[d[1]][d[2]]
                        else:
                            key = ("d", d[1])
                            val = 16 * d[2]
                        if val > need.get(key, 0):
                            need[key] = val
                    for key, val in need.items():
                        if waited.get(key, 0) >= val:
                            continue
                        waited[key] = val
                        sem = sem_e[key[1]] if key[0] == "e" else sem_c[key[1]]
                        engobj.wait_ge(sem, val)
                        nw += 1
                    if ins["meth"] is None:
                        continue
                    inst = getattr(engobj, ins["meth"])(*ins["args"], **ins["kw"])
                    if ins["chan"] is not None:
                        inst.then_inc(sem_c[ins["chan"]], 16)
                    elif ins["sig"]:
                        inst.then_inc(sem_e[e], 1)
                stats[e] = (len(self.ins[e]), nw)

            @block.tensor
            def _(eng):
                replay("pe", eng)

            @block.scalar
            def _(eng):
                replay("act", eng)

            @block.vector
            def _(eng):
                replay("dve", eng)

            @block.gpsimd
            def _(eng):
                replay("pool", eng)

            @block.sync
            def _(eng):
                replay("sp", eng)
        self.stats = stats


def build_program(layers=(0, 1), do_final=True, dbg=(), stop=None):
    nc = bass.Bass("TRN2", target_bir_lowering=False)

    def dram(name, shape, dt, kind="ExternalInput"):
        return nc.dram_tensor(name, list(shape), dt, kind=kind).ap()

    x_d = dram("x", [T, D], F32)
    mem_d = dram("mem", [MEM, D], F32)
    win_d = dram("w_in_r", [L * 8 * 128, 8 * 512], F32)
    wkv_d = dram("w_mem_kv", [L * 128, 8 * 512], F32)
    wout_d = dram("w_out", [L * 128, 8 * D], F32)
    ng_d = dram("norm_g", [L, D], F32)
    mg_d = dram("mem_norm_g", [L, D], F32)
    fg_d = dram("final_norm_g", [1, D], F32)
    ident_d = dram("c_ident", [128, 128], BF16)
    negtri_d = dram("c_negtri", [128, 128], BF16)
    negones_d = dram("c_negones", [128, 128], BF16)
    dmsb_d = dram("c_dmask_sb", [128, 128], BF16)
    dmmb_d = dram("c_dmask_mb", [128, 128], BF16)
    kind_d = dram("c_kind", [8, T], BF16)
    cc_d = dram("c_cc", [128, NT * 16], F32)
    ss_d = dram("c_ss", [128, NT * 16], F32)
    gbias_d = dram("c_gbias", [128, NT * 8], F32)
    mtile_d = dram("c_mtile", [128, NT * 8], F32)
    out_d = dram("out", [T, D], F32, kind="ExternalOutput")
    dbg_d = {}
    if "mixed" in dbg:
        dbg_d["mixed"] = dram("dbg_mixed", [T, D], BF16, kind="ExternalOutput")
    if "x1" in dbg:
        dbg_d["x1"] = dram("dbg_x1", [T, D], F32, kind="ExternalOutput")
    if "hT" in dbg:
        dbg_d["hT"] = dram("dbg_hT", [128, 8 * T], BF16, kind="ExternalOutput")

    with ExitStack() as es:
        def sb(name, shape, dt):
            return es.enter_context(nc.sbuf_tensor(name, list(shape), dt))

        xres = sb("xres", [128, NT, D], F32)
        hT = sb("hT", [128, 8, T], BF16)
        mixed = sb("mixed", [128, NT, D], BF16)
        A = [sb("A0", [128, T], BF16), sb("A1", [128, T], BF16)]
        B = [sb("B0", [128, T], BF16), sb("B1", [128, T], BF16)]
        vaug = sb("vaug", [128, NT, 2, 66], BF16)
        sg = sb("sg", [128, NT, 128], BF16)
        wbuf = sb("wbuf", [128, 2, 8, 512], BF16)
        HB = sb("HB", [128, 2, D], BF16)
        hb = [HB[:, 0, :], HB[:, 1, :]]
        vaug2 = HB[:].rearrange("p a (t d) -> p (a t) d", d=64).rearrange("p (t h) d -> p t h d", h=2)
        ework = sb("ework", [128, D], F32)
        ebuf = [ework[:, 0:512], ework[:, 512:1024]]
        spb = [sb("sp0", [128, 512], BF16), sb("sp1", [128, 512], BF16)]
        sacc = [sb("sacc0", [128, 512], BF16), sb("sacc1", [128, 512], BF16)]
        wbf = [sb("wbf0", [128, 512], BF16), sb("wbf1", [128, 512], BF16)]
        ident = sb("ident", [128, 128], BF16)
        negtri = sb("negtri", [128, 128], BF16)
        negones = sb("negones", [128, 128], BF16)
        dmask_sb = sb("dmask_sb", [128, 128], BF16)
        dmask_mb = sb("dmask_mb", [128, 128], BF16)
        CC = sb("CC", [128, NT, 16], F32)
        SS = sb("SS", [128, NT, 16], F32)
        gbias = sb("gbias", [128, NT, 8], F32)
        mtile = sb("mtile", [128, NT, 8], F32)
        gA = sb("gA", [128, D], F32)
        ssq = sb("ssq", [128, NT], F32)
        lnv = sb("lnv", [128, NT], F32)
        rstd = sb("rstd", [128, NT], F32)
        epsb = sb("epsb", [128, 1], F32)
        RQV = sb("RQV", [128, NT * 2 * 72], BF16)
        Rq = RQV[:].rearrange("p (t h d) -> p t h d", t=NT, h=2)
        Rk = [sb("Rk%d" % i, [128, 2, 64], BF16) for i in range(3)]
        Ub = [sb("U0", [128, 4, 16], F32), sb("U1", [128, 4, 16], F32)]
        Vb = [sb("V0", [128, 4, 16], F32), sb("V1", [128, 4, 16], F32)]
        qkraw = [sb("qkraw0", [128, 4, 64], F32), sb("qkraw1", [128, 4, 64], F32)]
        ksum = sb("ksum", [64, 2, 8], F32)
        ksum_bf = sb("ksum_bf", [64, 2, 8], BF16)
        gm = sb("gm", [128, 2, NT, 8], F32)
        mx = sb("mx", [128, NT, 8], F32)
        sel = sb("sel", [128, NT, 8], F32)
        MTS = sb("MTS", [128, 8 * MEM], BF16)
        mT = MTS[:].rearrange("p (c m) -> p c m", c=8)
        sg2 = MTS[:].rearrange("p (t n) -> p t n", t=NT)
        mkT = sb("mkT", [128, 2, MEM], BF16)
        mvaug = sb("mvaug", [128, 2, 4, 66], BF16)
        rden = [sb("rden0", [128, 4, 1], F32), sb("rden1", [128, 4, 1], F32)]
        ps = [es.enter_context(nc.psum_tensor("ps%d" % i, [128, 512], F32)) for i in range(6)]
        psb = [es.enter_context(nc.psum_tensor("psb%d" % i, [128, 1024], BF16)) for i in range(2)]

        p = Prog(nc)
        MMK = dict(skip_group_check=True)

        for t0 in range(0, NT, 4):
            src = x_d[t0 * 128:(t0 + 4) * 128, :].rearrange("(t p) d -> p t d", p=128)
            p.dma("x%d" % t0, xres[:, t0:t0 + 4, :], src, writes=[("x", t) for t in range(t0, t0 + 4)])
        p.dma("c_ident", ident[:], ident_d, writes=["ident"])
        p.dma("c_negtri", negtri[:], negtri_d, writes=["negtri"])
        p.dma("c_negones", negones[:], negones_d, writes=["negones"])
        p.dma("c_dmsb", dmask_sb[:], dmsb_d, writes=["dmask_sb"])
        p.dma("c_dmmb", dmask_mb[:], dmmb_d, writes=["dmask_mb"])
        p.dma("c_cc", CC[:].rearrange("p t i -> p (t i)"), cc_d, writes=["CC"])
        p.dma("c_ss", SS[:].rearrange("p t i -> p (t i)"), ss_d, writes=["SS"])
        p.dma("c_gbias", gbias[:].rearrange("p t i -> p (t i)"), gbias_d, writes=["gbias"])
        p.dma("c_mtile", mtile[:].rearrange("p t i -> p (t i)"), mtile_d, writes=["mtile"])
        p.op("pool", "memset", epsb[:], EPS, writes=["epsb"])
        import os
        if "MEMSETMIXED" in os.environ:
            p.op("pool", "memset", mixed[:], 0.0, writes=[("mx", t) for t in range(NT)])
        p.op("pool", "memset", vaug[:, :, :, 64:65], 1.0, writes=[("vone",)])
        p.op("pool", "memset", mvaug[:, :, :, 64:65], 1.0, writes=[("mvone",)])

        wq = {"n": 0}

        def load_w(src_rows, ncols=512, slot=None):
            if slot is None:
                slot = wq["n"] % 2
                wq["n"] += 1
            p.dma("w%d" % slot, wbuf[:, slot].rearrange("p c n -> p (c n)"), src_rows, eng="pool",
                  writes=[("w", slot)])
            return slot

        def rmsnorm_stats(src_fn, res_fn, n):
            for t in range(n):
                p.op("dve", "scalar_tensor_tensor", out=hb[1][:], in0=src_fn(t), scalar=1.0, in1=src_fn(t),
                     op0=ALU.mult, op1=ALU.mult, accum_out=ssq[:, t:t + 1],
                     reads=res_fn(t), writes=[("ssq", t), ("hb", 1)])
            p.op("act", "activation", out=lnv[:, 0:n], in_=ssq[:, 0:n], func=AF.Ln, scale=1.0 / D, bias=epsb[:],
                 reads=[("ssq", t) for t in range(n)] + ["epsb"], writes=["lnv"])
            p.op("act", "activation", out=rstd[:, 0:n], in_=lnv[:, 0:n], func=AF.Exp, scale=-0.5,
                 reads=["lnv"], writes=["rstd"])

        def norm_transpose(src_fn, res_fn, n, dstT, dst_res_fn, dst_t0=0):
            rmsnorm_stats(src_fn, res_fn, n)
            for t in range(n):
                i = t % 2
                dt_ = dst_t0 + t
                p.op("dve", "scalar_tensor_tensor", out=hb[i][:], in0=src_fn(t), scalar=rstd[:, t:t + 1],
                     in1=gA[:], op0=ALU.mult, op1=ALU.mult,
                     reads=res_fn(t) + ["rstd", "gA"], writes=[("hb", i)])
                for c in range(8):
                    p.op("pe", "transpose", out=psb[i][:, c * 128:(c + 1) * 128], in_=hb[i][:, c * 128:(c + 1) * 128],
                         identity=ident[:], reads=[("hb", i), "ident"], writes=[("psb", i)])
                p.op("act", "activation", out=dstT[:, :, dt_ * 128:(dt_ + 1) * 128],
                     in_=psb[i][:].rearrange("p (c n) -> p c n", n=128), func=AF.Copy,
                     reads=[("psb", i)], writes=[dst_res_fn(dt_)])

        obank = {"n": 0}

        def attn_head(*a, **k):
            for _ in attn_head_gen(*a, **k):
                pass

        def attn_head_gen(kind, causal, q_ap, q_res, k_ap, k_res, v_ap, v_res, nkeys_tiles, dmask, dmask_res,
                          c_off, hl, exp_scale, sgt=None, sg_res=None):
            yield from attn_stream_gen(kind, [dict(causal=causal, q_ap=q_ap, q_res=q_res, k_ap=k_ap, k_res=k_res,
                                                   v_ap=v_ap, v_res=v_res, nkeys_tiles=nkeys_tiles, dmask=dmask,
                                                   dmask_res=dmask_res, c_off=c_off, hl=hl, exp_scale=exp_scale,
                                                   sgt=sgt, sg_res=sg_res)])

        def attn_stream_gen(kind, heads, side=None, side_every=1, side_burst=1):
            recs = []
            for H in heads:
                if H.get("sgt") is None:
                    H["sgt"] = sg
                    H["sg_res"] = lambda t: ("sg", 0, t)
                for G in range(4):
                    if H["causal"]:
                        kts = list(range(4 * G + 3, -1, -1)) if kind == "sb" else list(range(0, 4 * G + 4))
                    else:
                        kts = list(range(H["nkeys_tiles"]))
                    ob = 4 + (obank["n"] % 2)
                    obank["n"] += 1
                    grp = dict(H=H, G=G, ob=ob, first_pv=True)
                    for i, kt in enumerate(kts):
                        if H["causal"]:
                            j = max(kt - 4 * G, 0)
                            c0, dg = j * 128, kt >= 4 * G
                        else:
                            c0, dg = 0, False
                        recs.append(dict(grp=grp, i=i, kt=kt, c0=c0, dg=dg, last=(i == len(kts) - 1)))
            F = len(recs)

            def stageA(f):
                r = recs[f]; H = r["grp"]["H"]; G = r["grp"]["G"]
                kt, c0, dg = r["kt"], r["c0"], r["dg"]
                bank = f % 2
                p.op("pe", "matmul", ps[bank][:, c0:512], H["k_ap"](kt), H["q_ap"](G, c0), start=True, stop=not dg,
                     reads=[H["k_res"](kt), H["q_res"](G)], writes=[("ps", bank)], **MMK)
                if dg:
                    p.op("pe", "matmul", ps[bank][:, c0:c0 + 128], ident[:], H["dmask"][:], start=False, stop=True,
                         reads=["ident", H["dmask_res"]], writes=[("ps", bank)], **MMK)

            def stageB1(f):
                r = recs[f]; H = r["grp"]["H"]
                c0 = r["c0"]
                bank = f % 2
                if kind == "sb":
                    p.op("act", "activation", out=ebuf[bank][:, c0:512], in_=ps[bank][:, c0:512], func=AF.Exp,
                         reads=[("ps", bank)], writes=[("e", bank)])
                else:
                    p.op("act", "activation", out=wbf[bank][:, c0:512], in_=ps[bank][:, c0:512], func=AF.Exp,
                         scale=H["exp_scale"], reads=[("ps", bank)], writes=[("wbf", bank)])

            def stageB2(f):
                c0 = recs[f]["c0"]
                bank = f % 2
                p.op("act", "activation", out=spb[bank][:, c0:512], in_=ebuf[bank][:, c0:512], func=AF.Ln,
                     bias=1.0, reads=[("e", bank)], writes=[("sp", bank)])

            def sacc_update(f):
                pc0 = recs[f - 1]["c0"]
                bank = f % 2
                pb = (f - 1) % 2
                p.op("dve", "tensor_tensor", out=sacc[bank][:, pc0:512], in0=sacc[pb][:, pc0:512],
                     in1=spb[pb][:, pc0:512], op=ALU.add,
                     reads=[("sacc", pb), ("sp", pb)], writes=[("sacc", bank)])

            def stageC(f):
                r = recs[f]; H = r["grp"]["H"]; G = r["grp"]["G"]
                kt, c0, dg, i = r["kt"], r["c0"], r["dg"], r["i"]
                bank = f % 2
                pb = 2 + bank
                p.op("pe", "matmul", ps[pb][:, c0:512], H["k_ap"](kt), H["q_ap"](G, c0), start=True, stop=False,
                     reads=[H["k_res"](kt), H["q_res"](G)], writes=[("ps", pb)], **MMK)
                last = (i == 0) and (not dg)
                p.op("pe", "matmul", ps[pb][:, c0:512], negtri[:], spb[bank][:, c0:512], start=False, stop=last,
                     reads=["negtri", ("sp", bank)], writes=[("ps", pb)], **MMK)
                if i >= 1:
                    p.op("pe", "matmul", ps[pb][:, c0:512], negones[:], sacc[bank][:, c0:512], start=False,
                         stop=not dg, reads=["negones", ("sacc", bank)], writes=[("ps", pb)], **MMK)
                if dg:
                    p.op("pe", "matmul", ps[pb][:, c0:c0 + 128], ident[:], H["dmask"][:], start=False, stop=True,
                         reads=["ident", H["dmask_res"]], writes=[("ps", pb)], **MMK)

            def stageD(f):
                c0 = recs[f]["c0"]
                bank = f % 2
                p.op("act", "activation", out=wbf[bank][:, c0:512], in_=ps[2 + bank][:, c0:512], func=AF.Exp,
                     reads=[("ps", 2 + bank)], writes=[("wbf", bank)])

            def stageE(f):
                r = recs[f]; grp = r["grp"]; H = grp["H"]
                kt, c0 = r["kt"], r["c0"]
                bank = f % 2
                ob = grp["ob"]
                nv = 64 if kind == "sb" else 65
                for ql in range(c0 // 128, 4):
                    p.op("pe", "matmul", ps[ob][:, ql * 66:ql * 66 + nv], wbf[bank][:, ql * 128:(ql + 1) * 128],
                         H["v_ap"](kt), start=grp["first_pv"], stop=False,
                         reads=[("wbf", bank), H["v_res"](kt)], writes=[("ps", ob)], **MMK)
                    grp["first_pv"] = False
                if r["last"]:
                    evac(grp)

            def evac(grp):
                H = grp["H"]; G = grp["G"]; ob = grp["ob"]
                sgt, sg_res, c_off, hl = H["sgt"], H["sg_res"], H["c_off"], H["hl"]
                rd = rden[ob - 4]
                pov = ps[ob][:, 0:264].rearrange("p (a b) -> p a b", b=66)
                if kind == "sb":
                    p.op("dve", "tensor_tensor", out=mixed[:, 4 * G:4 * G + 4, c_off:c_off + 64], in0=pov[:, :, 0:64],
                         in1=sgt[:, 4 * G:4 * G + 4, hl * 64:(hl + 1) * 64], op=ALU.mult,
                         reads=[("ps", ob)] + [sg_res(t) for t in range(4 * G, 4 * G + 4)],
                         writes=[("mx", t) for t in range(4 * G, 4 * G + 4)])
                else:
                    p.op("dve", "reciprocal", out=rd[:], in_=pov[:, :, 64:65], reads=[("ps", ob)],
                         writes=[("rden", ob)])
                    for ql in range(4):
                        t = 4 * G + ql
                        p.op("dve", "scalar_tensor_tensor", out=mixed[:, t, c_off:c_off + 64],
                             in0=ps[ob][:, ql * 66:ql * 66 + 64], scalar=rd[:, ql, :],
                             in1=sgt[:, t, hl * 64:(hl + 1) * 64], op0=ALU.mult, op1=ALU.mult,
                             reads=[("ps", ob), ("rden", ob), sg_res(t)], writes=[("mx", t)])

            def side_step(it=0):
                nonlocal side
                if side is not None and it % side_every == side_every - 1:
                    try:
                        for _ in range(side_burst):
                            next(side)
                    except StopIteration:
                        side = None

            if kind == "sb":
                stageA(0)
                stageB1(0)
                for it in range(F + 2):
                    if it + 1 < F:
                        stageA(it + 1)
                    if it < F:
                        stageB2(it)
                    if 0 <= it - 1 < F:
                        stageC(it - 1)
                    if it + 1 < F and recs[it + 1]["i"] >= 1:
                        if recs[it + 1]["i"] == 1:
                            p.op("pool", "memset", sacc[0][:], 0.0, writes=[("sacc", 0)])
                            p.op("pool", "memset", sacc[1][:], 0.0, writes=[("sacc", 1)])
                        sacc_update(it + 1)
                    if it + 1 < F:
                        stageB1(it + 1)
                    if 0 <= it - 1 < F:
                        stageD(it - 1)
                    if SIDE_MID:
                        side_step(it)
                    if 0 <= it - 2 < F:
                        stageE(it - 2)
                    if not SIDE_MID:
                        side_step(it)
                    yield
            else:
                for it in range(F + 1):
                    if it < F:
                        stageA(it)
                        stageB1(it)
                    side_step(it)
                    if 0 <= it - 1 < F:
                        stageE(it - 1)
                    yield
            if side is not None:
                for _ in side:
                    pass

        dumped = set()
        fused_norm_done = False
        final_done = False
        for li, l in enumerate(layers):
            wrow = lambda g: win_d[(l * 8 + g) * 128:(l * 8 + g + 1) * 128, :]
            if not fused_norm_done:
                p.dma("gA", gA[:], ng_d[l:l + 1, :].partition_broadcast(128), writes=["gA"])
            slot_next = load_w(wrow(0))
            if not fused_norm_done:
                norm_transpose(lambda t: xres[:, t, :], lambda t: [("x", t)], NT, hT, lambda t: ("hT", t))
            fused_norm_done = False
            if "hT" in dbg and li == 0:
                p.dma("dbg", dbg_d["hT"], hT[:].rearrange("p c t -> p (c t)"), reads=[("hT", t) for t in range(NT)])

            if stop in ("norm", "normfinal"):
                break
            VS = [vaug, vaug2]
            SGS = [sg, sg2]
            p.alias_transfer([("hb", 0), ("hb", 1)], [("v", 1, t) for t in range(NT)])
            p.alias_transfer([("mT", 0), ("mT", 1)], [("sg", 1, t) for t in range(NT)])

            def sb_proj_gen(pr, slot):
                st = (pr + 1) % 2
                k = 0
                for tg in range(4):
                    for which in range(2):
                        bi = k % 2
                        k += 1
                        pbank = psb[bi][:].bitcast(F32)
                        for c in range(8):
                            p.op("pe", "matmul", pbank, wbuf[:, slot, c, which * 128:(which + 1) * 128],
                                 hT[:, c, tg * 512:(tg + 1) * 512], start=(c == 0), stop=(c == 7),
                                 reads=[("w", slot)] + [("hT", t) for t in range(4 * tg, 4 * tg + 4)],
                                 writes=[("psb", bi)], **MMK)
                            if c == 3:
                                yield
                        if which == 0:
                            p.op("dve", "tensor_scalar", out=A[st][:, tg * 512:(tg + 1) * 512], in0=pbank,
                                 scalar1=0.125, scalar2=None, op0=ALU.mult,
                                 reads=[("psb", bi)], writes=[("A", st, tg)])
                        else:
                            p.op("dve", "tensor_copy", out=B[st][:, tg * 512:(tg + 1) * 512], in_=pbank,
                                 reads=[("psb", bi)], writes=[("B", st, tg)])
                        yield
                for t in range(NT):
                    bi = k % 2
                    k += 1
                    pbank = psb[bi][:].bitcast(F32)
                    for c in range(8):
                        p.op("pe", "matmul", pbank[:, 0:256], hT[:, c, t * 128:(t + 1) * 128],
                             wbuf[:, slot, c, 256:512], start=(c == 0), stop=(c == 7),
                             reads=[("w", slot), ("hT", t)], writes=[("psb", bi)], **MMK)
                    p.op("dve", "tensor_copy", out=VS[st][:, t, :, 0:64],
                         in_=pbank[:, 0:128].rearrange("p (h d) -> p h d", d=64),
                         reads=[("psb", bi)], writes=[("v", st, t)])
                    p.op("dve", "tensor_copy", out=SGS[st][:, t, :], in_=pbank[:, 128:256],
                         reads=[("psb", bi)], writes=[("sg", st, t)])
                    yield
                sgflat = SGS[st][:].rearrange("p t n -> p (t n)") if st == 0 else MTS[:]
                p.op("act", "activation", out=sgflat, in_=sgflat, func=AF.Silu,
                     reads=[("sg", st, t) for t in range(NT)], writes=[("sg", st, t) for t in range(NT)])
                yield

            def sb_attn_gen(pr, side=None):
                st = (pr + 1) % 2
                heads = []
                for hl in range(2):
                    r0, r1 = hl * 64, (hl + 1) * 64
                    heads.append(dict(
                        causal=True,
                        q_ap=lambda G, c0, r0=r0, r1=r1: A[st][r0:r1, G * 512 + c0:(G + 1) * 512],
                        q_res=lambda G: ("A", st, G),
                        k_ap=lambda kt, r0=r0, r1=r1: B[st][r0:r1, kt * 128:(kt + 1) * 128],
                        k_res=lambda kt: ("B", st, kt // 4),
                        v_ap=lambda kt, hl=hl: VS[st][:, kt, hl, 0:64],
                        v_res=lambda kt: ("v", st, kt),
                        nkeys_tiles=None, dmask=dmask_sb, dmask_res="dmask_sb",
                        c_off=(2 * pr + hl) * 64, hl=hl, exp_scale=1.0,
                        sgt=SGS[st], sg_res=lambda t: ("sg", st, t)))
                yield from attn_stream_gen("sb", heads, side=side, side_every=int(os.environ.get("KSBE", "3")),
                                           side_burst=int(os.environ.get("KSBB", "1")))

            def interleave(main, side, every, first):
                i = 0
                side_live = side is not None
                for _ in main:
                    i += 1
                    if side_live and i >= first and (i - first) % every == 0:
                        try:
                            next(side)
                        except StopIteration:
                            side_live = False
                if side_live:
                    for _ in side:
                        pass

            def moba_setup(hl):
                p.op("pool", "memset", B[hl][64:96, :], 0.0, writes=[("B", hl, tg) for tg in range(4)])
                p.op("pool", "memset", A[hl][64:96, :], 0.0, writes=[("A", hl, tg) for tg in range(4)])
                p.dma("kind%d" % hl, B[hl][64:72, :], kind_d, writes=[("B", hl, tg) for tg in range(4)])

            mbs = {"slot_next": None, "slots": {}}

            def mb_proj_gen(u, early=False):
                pr, hl = u // 2, u % 2
                PF = psb[0][:].bitcast(F32)

                def mm_out(t):
                    return (PF[:, 0:256], ("psb", 0)) if early else (ps[2 + t % 2][:, 0:256], ("ps", 2 + t % 2))

                def tq(t):
                    col = (t % 4) * 128
                    return (psb[1][0:64, col:col + 128], ("psb", 1)) if early else (psb[0][0:64, col:col + 128], ("psb", 0))

                def tk(t):
                    col = (t % 4) * 128
                    return (psb[1][0:64, 512 + col:512 + col + 128], ("psb", 1)) if early else (psb[1][0:64, col:col + 128], ("psb", 1))
                if hl == 0:
                    mbs["slots"][pr] = mbs["slot_next"]
                    mbs["slot_next"] = load_w(wrow(3 + pr + 1)) if pr < 2 else load_w(wrow(6), 256)
                slot = mbs["slots"][pr]

                def wv(c):
                    return wbuf[:, slot, c, :].rearrange("p (k h d) -> p k h d", k=4, h=2)[:, :, hl, :]

                def stage1(t):
                    mo, mres = mm_out(t)
                    i = t % 2
                    j = t % 3
                    for c in range(8):
                        p.op("pe", "matmul", mo, hT[:, c, t * 128:(t + 1) * 128], wv(c),
                             start=(c == 0), stop=(c == 7), reads=[("w", slot), ("hT", t)],
                             writes=[mres], **MMK)
                        if c == 3:
                            yield
                    pv4 = mo.rearrange("p (a b) -> p a b", b=64)
                    if early:
                        p.op("dve", "tensor_copy", out=qkraw[i][:, 0:2, :], in_=pv4[:, 0:2, :],
                             reads=[mres], writes=[("qkraw", i)])
                    else:
                        p.op("act", "activation", out=qkraw[i][:, 0:2, :], in_=pv4[:, 0:2, :], func=AF.Copy,
                             reads=[mres], writes=[("qkraw", i)])
                    p.op("dve", "tensor_copy", out=vaug[:, t, hl, 0:64], in_=mo[:, 128:192],
                         reads=[mres], writes=[("vh", t, hl)])
                    p.op("dve", "tensor_copy", out=sg[:, t, hl * 64:(hl + 1) * 64], in_=mo[:, 192:256],
                         reads=[mres], writes=[("sgh", t, hl)])
                    p.op("pool", "tensor_tensor", out=Ub[i][:, 0:2, :], in0=qkraw[i][:, 0:2, 0:16],
                         in1=CC[:, t:t + 1, :].to_broadcast([128, 2, 16]), op=ALU.mult,
                         reads=[("qkraw", i), "CC"], writes=[("U", i)])
                    p.op("pool", "tensor_tensor", out=Vb[i][:, 0:2, :], in0=qkraw[i][:, 0:2, 0:16],
                         in1=SS[:, t:t + 1, :].to_broadcast([128, 2, 16]), op=ALU.mult,
                         reads=[("qkraw", i), "SS"], writes=[("V", i)])
                    p.op("pool", "tensor_tensor", out=Rq[:, t, hl, 0:8], in0=Ub[i][:, 0, 0:8], in1=Vb[i][:, 0, 8:16],
                         op=ALU.subtract, reads=[("U", i), ("V", i)], writes=[("Rq_rope", t, hl)])
                    p.op("pool", "tensor_tensor", out=Rq[:, t, hl, 8:16], in0=Ub[i][:, 0, 8:16], in1=Vb[i][:, 0, 0:8],
                         op=ALU.add, reads=[("U", i), ("V", i)], writes=[("Rq_rope", t, hl)])
                    p.op("pool", "tensor_tensor", out=Rk[j][:, 0, 0:8], in0=Ub[i][:, 1, 0:8], in1=Vb[i][:, 1, 8:16],
                         op=ALU.subtract, reads=[("U", i), ("V", i)], writes=[("Rk_rope", j)])
                    p.op("pool", "tensor_tensor", out=Rk[j][:, 0, 8:16], in0=Ub[i][:, 1, 8:16], in1=Vb[i][:, 1, 0:8],
                         op=ALU.add, reads=[("U", i), ("V", i)], writes=[("Rk_rope", j)])
                    p.op("dve", "tensor_copy", out=Rq[:, t, hl, 16:64], in_=qkraw[i][:, 0, 16:64],
                         reads=[("qkraw", i)], writes=[("Rq_rest", t, hl)])
                    p.op("dve", "tensor_copy", out=Rk[j][:, 0, 16:64], in_=qkraw[i][:, 1, 16:64],
                         reads=[("qkraw", i)], writes=[("Rk_rest", j)])

                def stage2(t):
                    j = t % 3
                    tg = t // 4
                    qo, qres = tq(t)
                    ko, kres = tk(t)
                    p.op("pe", "transpose", out=qo, in_=Rq[:, t, hl, 0:64],
                         identity=ident[:], reads=[("Rq_rope", t, hl), ("Rq_rest", t, hl), "ident"],
                         writes=[qres])
                    p.op("pe", "transpose", out=ko, in_=Rk[j][:, 0, :],
                         identity=ident[:], reads=[("Rk_rope", j), ("Rk_rest", j), "ident"],
                         writes=[kres])
                    if t % 4 == 3:
                        qsrc = psb[1][0:64, 0:512] if early else psb[0][0:64, 0:512]
                        ksrc = psb[1][0:64, 512:1024] if early else psb[1][0:64, 0:512]
                        p.op("act", "activation", out=A[hl][0:64, tg * 512:(tg + 1) * 512],
                             in_=qsrc, func=AF.Copy,
                             reads=[qres], writes=[("A", hl, tg)])
                        p.op("dve", "tensor_copy", out=B[hl][0:64, tg * 512:(tg + 1) * 512],
                             in_=ksrc,
                             reads=[kres], writes=[("B", hl, tg)])

                for tt_ in range(NT + 2):
                    if tt_ < NT:
                        yield from stage1(tt_)
                    if tt_ - 2 >= 0:
                        stage2(tt_ - 2)
                    yield
                sgh = sg[:, :, hl * 64:(hl + 1) * 64]
                p.op("act", "activation", out=sgh, in_=sgh, func=AF.Silu,
                     reads=[("sgh", t, hl) for t in range(NT)], writes=[("sgh", t, hl) for t in range(NT)])
                yield
                p.op("dve", "tensor_reduce", out=ksum[:, hl, :],
                     in_=B[hl][0:64, :].rearrange("p (n k) -> p n k", k=256), axis=AX.X, op=ALU.add,
                     reads=[("B", hl, tg) for tg in range(4)], writes=[("ksum", hl)])
                p.op("dve", "tensor_copy", out=ksum_bf[:, hl, :], in_=ksum[:, hl, :],
                     reads=[("ksum", hl)], writes=[("ksum_bf", hl)])
                for t in range(NT):
                    col = t * 8
                    gdst = PF[:, 256 + col:256 + col + 8] if early else ps[2][:, col:col + 8]
                    p.op("pe", "matmul", gdst, A[hl][0:64, t * 128:(t + 1) * 128],
                         ksum_bf[:, hl, :], start=True, stop=True,
                         reads=[("A", hl, t // 4), ("ksum_bf", hl)],
                         writes=[("psb", 0) if early else ("ps", 2)], **MMK)
                gsrc = PF[:, 256:384] if early else ps[2][:, 0:128]
                gview = gsrc.rearrange("p (t n) -> p t n", n=8)
                p.op("dve", "tensor_tensor", out=gm[:, hl], in0=gview, in1=gbias[:], op=ALU.add,
                     reads=[("psb", 0) if early else ("ps", 2), "gbias"], writes=[("gm", hl)])
                yield
                for t in range(NT):
                    p.op("dve", "max", out=mx[:, t, :], in_=gm[:, hl, t, :], reads=[("gm", hl)],
                         writes=[("mx8",)])
                p.op("dve", "tensor_tensor", out=sel[:], in0=gm[:, hl],
                     in1=mx[:, :, 2:3].to_broadcast([128, NT, 8]), op=ALU.is_ge,
                     reads=[("gm", hl), ("mx8",)], writes=[("sel",)])
                p.op("dve", "scalar_tensor_tensor", out=Rq[:, :, hl, 64:72], in0=sel[:], scalar=-1.0,
                     in1=mtile[:], op0=ALU.add, op1=ALU.mult,
                     reads=[("sel",), "mtile"], writes=[("Rq_nm", hl)])
                yield
                for half in range(2):
                    nb = 1 if early else half
                    for tt in range(8):
                        t = half * 8 + tt
                        p.op("pe", "transpose", out=psb[nb][0:72, tt * 128:(tt + 1) * 128],
                             in_=Rq[:, t, hl, 0:72], identity=ident[:],
                             reads=[("Rq_rope", t, hl), ("Rq_rest", t, hl), ("Rq_nm", hl), "ident"],
                             writes=[("psb", nb)])
                    if half == 0:
                        p.op("act", "activation", out=A[hl][64:72, half * 1024:(half + 1) * 1024],
                             in_=psb[nb][64:72, :], func=AF.Copy, reads=[("psb", nb)],
                             writes=[("A", hl, 2 * half), ("A", hl, 2 * half + 1)])
                    else:
                        p.op("dve", "tensor_copy", out=A[hl][64:72, half * 1024:(half + 1) * 1024],
                             in_=psb[nb][64:72, :], reads=[("psb", nb)],
                             writes=[("A", hl, 2 * half), ("A", hl, 2 * half + 1)])
                    yield

            def mb_head(u):
                pr, hl = u // 2, u % 2
                return dict(
                    causal=True,
                    q_ap=lambda G, c0: A[hl][0:96, G * 512 + c0:(G + 1) * 512],
                    q_res=lambda G: ("A", hl, G),
                    k_ap=lambda kt: B[hl][0:96, kt * 128:(kt + 1) * 128],
                    k_res=lambda kt: ("B", hl, kt // 4),
                    v_ap=lambda kt: vaug[:, kt, hl, 0:65],
                    v_res=lambda kt: ("vh", kt, hl),
                    nkeys_tiles=None, dmask=dmask_mb, dmask_res="dmask_mb",
                    c_off=384 + u * 64, hl=hl, exp_scale=0.125,
                    sgt=sg, sg_res=lambda t: ("sgh", t, hl))

            nsb = 3 if stop not in ("sb1", "sbproj") else 1
            moba_early = False
            slot = slot_next
            slot_next = load_w(wrow(1))
            for _ in sb_proj_gen(0, slot):
                pass
            for pr in range(nsb):
                side = None
                if pr + 1 < nsb:
                    slot = slot_next
                    side = sb_proj_gen(pr + 1, slot)
                elif nsb == 3 and stop is None:
                    p.alias_transfer([("v", 0, t) for t in range(NT)],
                                     [("vh", t, h) for t in range(NT) for h in range(2)])
                    p.alias_transfer([("sg", 0, t) for t in range(NT)],
                                     [("sgh", t, h) for t in range(NT) for h in range(2)])
                    moba_setup(0)
                    mbs["slot_next"] = slot_next
                    side = mb_proj_gen(0, early=True)
                    moba_early = True
                if stop == "sbproj":
                    break
                for _ in sb_attn_gen(pr, side):
                    pass
                if pr + 1 < 3:
                    slot_next = load_w(wrow(pr + 2))
            p.alias_transfer([("v", 1, t) for t in range(NT)], [("hb", 0), ("hb", 1)])
            p.alias_transfer([("sg", 1, t) for t in range(NT)], [("mT", 0), ("mT", 1)])
            if stop in ("sb", "sb1", "sbproj"):
                break
            SGM = [sg2, sg]
            SGN = [1, 0]
            memst = {}

            def mem_pre_gen():
                slot_q = [mbs["slot_next"], None]
                memst["slot_q"] = slot_q
                slot_kv = load_w(wkv_d[l * 128:(l + 1) * 128, :])
                p.dma("gA", gA[:], mg_d[l:l + 1, :].partition_broadcast(128), writes=["gA"])
                for mt in range(2):
                    p.dma("memraw", ework[:], mem_d[mt * 128:(mt + 1) * 128, :], writes=[("e", 0), ("e", 1)])
                    norm_transpose(lambda t: ework[:], lambda t: [("e", 0), ("e", 1)], 1, mT, lambda t: ("mT", t),
                                   dst_t0=mt)
                    yield
                for pp in range(2):
                    bank = 2 + pp
                    for c in range(8):
                        p.op("pe", "matmul", ps[bank][:, 0:256], wbuf[:, slot_kv, c, pp * 128:(pp + 1) * 128],
                             mT[:, c, :], start=(c == 0), stop=(c == 7),
                             reads=[("w", slot_kv), ("mT", 0), ("mT", 1)], writes=[("ps", bank)], **MMK)
                    p.op("dve", "tensor_copy", out=mkT[:, pp, :], in_=ps[bank][:, 0:256], reads=[("ps", bank)],
                         writes=[("mkT", pp)])
                    yield
                for mt in range(2):
                    bank = 2 + mt
                    for c in range(8):
                        p.op("pe", "matmul", ps[bank][:, 0:256], mT[:, c, mt * 128:(mt + 1) * 128],
                             wbuf[:, slot_kv, c, 256:512], start=(c == 0), stop=(c == 7),
                             reads=[("w", slot_kv), ("mT", mt)], writes=[("ps", bank)], **MMK)
                    p.op("dve", "tensor_copy", out=mvaug[:, mt, :, 0:64],
                         in_=ps[bank][:, 0:256].rearrange("p (h d) -> p h d", d=64), reads=[("ps", bank)],
                         writes=[("mv", mt)])
                    yield
                slot_q[1] = load_w(wrow(7), 256, slot=slot_kv)
                p.alias_transfer([("mT", 0), ("mT", 1)], [("sg", 1, t) for t in range(NT)])
                yield from mem_proj_gen(0)

            def mem_proj_gen(pp):
                slot = memst["slot_q"][pp]
                for tg in range(4):
                    bank = 2 + tg % 2
                    for c in range(8):
                        p.op("pe", "matmul", ps[bank][:, :], wbuf[:, slot, c, 0:128],
                             hT[:, c, tg * 512:(tg + 1) * 512], start=(c == 0), stop=(c == 7),
                             reads=[("w", slot)] + [("hT", t) for t in range(4 * tg, 4 * tg + 4)],
                             writes=[("ps", bank)], **MMK)
                        if c == 3:
                            yield
                    p.op("act", "activation", out=A[pp][:, tg * 512:(tg + 1) * 512], in_=ps[bank][:, :],
                         func=AF.Copy, reads=[("ps", bank)], writes=[("A", pp, tg)])
                    yield
                for t in range(NT):
                    bank = 2 + t % 2
                    for c in range(8):
                        p.op("pe", "matmul", ps[bank][:, 0:128], hT[:, c, t * 128:(t + 1) * 128],
                             wbuf[:, slot, c, 128:256], start=(c == 0), stop=(c == 7),
                             reads=[("w", slot), ("hT", t)], writes=[("ps", bank)], **MMK)
                    p.op("dve", "tensor_copy", out=SGM[pp][:, t, :], in_=ps[bank][:, 0:128],
                         reads=[("ps", bank)], writes=[("sg", SGN[pp], t)])
                    yield
                sgf = MTS[:] if pp == 0 else sg[:].rearrange("p t n -> p (t n)")
                p.op("act", "activation", out=sgf, in_=sgf, func=AF.Silu,
                     reads=[("sg", SGN[pp], t) for t in range(NT)], writes=[("sg", SGN[pp], t) for t in range(NT)])
                yield

            def mem_heads(pp):
                heads = []
                for hl in range(2):
                    r0, r1 = hl * 64, (hl + 1) * 64
                    heads.append(dict(
                        causal=False,
                        q_ap=lambda G, c0, r0=r0, r1=r1: A[pp][r0:r1, G * 512 + c0:(G + 1) * 512],
                        q_res=lambda G: ("A", pp, G),
                        k_ap=lambda kt, r0=r0, r1=r1: mkT[r0:r1, pp, kt * 128:(kt + 1) * 128],
                        k_res=lambda kt: ("mkT", pp),
                        v_ap=lambda kt, hl=hl: mvaug[:, kt, 2 * pp + hl, 0:65],
                        v_res=lambda kt: ("mv", kt),
                        nkeys_tiles=2, dmask=None, dmask_res=None,
                        c_off=768 + (2 * pp + hl) * 64, hl=hl, exp_scale=0.125,
                        sgt=SGM[pp], sg_res=lambda t: ("sg", SGN[pp], t)))
                return heads

            if moba_early:
                moba_setup(1)
            else:
                p.alias_transfer([("v", 0, t) for t in range(NT)], [("vh", t, h) for t in range(NT) for h in range(2)])
                p.alias_transfer([("sg", 0, t) for t in range(NT)], [("sgh", t, h) for t in range(NT) for h in range(2)])
                moba_setup(0)
                moba_setup(1)
                mbs["slot_next"] = slot_next
                for _ in mb_proj_gen(0):
                    pass
            for u in range(6):
                side = mb_proj_gen(u + 1) if u + 1 < 6 else (mem_pre_gen() if stop is None else None)
                for _ in attn_stream_gen("sm", [mb_head(u)], side=side, side_every=int(os.environ.get("KMBE", "1")),
                                         side_burst=int(os.environ.get("KMBB", "1"))):
                    pass
            slot_next = mbs["slot_next"]
            p.alias_transfer([("sgh", t, h) for t in range(NT) for h in range(2)], [("sg", 0, t) for t in range(NT)])
            p.alias_transfer([("vh", t, h) for t in range(NT) for h in range(2)], [("v", 0, t) for t in range(NT)])
            if stop == "moba":
                break
            if stop is not None:
                for _ in mem_pre_gen():
                    pass
            for _ in attn_stream_gen("sm", mem_heads(0), side=mem_proj_gen(1), side_every=1, side_burst=4):
                pass
            for _ in attn_stream_gen("sm", mem_heads(1)):
                pass
            p.alias_transfer([("sg", 1, t) for t in range(NT)], [("mT", 0), ("mT", 1)])

            if "mixed" in dbg and li == int(os.environ.get("DBGLAYER", "0")):
                dumped.add("mixed")
                for t0 in range(0, NT, 4):
                    p.dma("dbg", dbg_d["mixed"][t0 * 128:(t0 + 4) * 128, :].rearrange("(t p) d -> p t d", p=128),
                          mixed[:, t0:t0 + 4, :], reads=[("mx", t) for t in range(t0, t0 + 4)])

            if stop == "mem":
                break
            wout_v = wbuf[:].rearrange("p s c n -> p (s c n)").rearrange("p (c n) -> p c n", n=D)
            wq["n"] = 0
            for sl in range(2):
                p.dma("w%d" % sl, wbuf[:, sl].rearrange("p c n -> p (c n)"),
                      wout_d[l * 128:(l + 1) * 128, sl * 4096:(sl + 1) * 4096], eng="pool", writes=[("w", sl)])
            dbg_x1 = "x1" in dbg and li == int(os.environ.get("DBGLAYER", "0"))
            if li + 1 < len(layers) and not dbg_x1:
                nmode = "layer"
                p.dma("gA", gA[:], ng_d[layers[li + 1]:layers[li + 1] + 1, :].partition_broadcast(128), writes=["gA"])
            elif li + 1 == len(layers) and do_final and stop is None and not dbg_x1:
                nmode = "final"
                p.dma("gA", gA[:], fg_d[0:1, :].partition_broadcast(128), writes=["gA"])
            else:
                nmode = "none"
            p.alias_transfer(["lnv", "rstd"], [("lnv_t", t) for t in range(NT)] + [("rstd_t", t) for t in range(NT)])
            junkb = ework[:].bitcast(BF16)[:, 0:D]

            def stT(t):
                for c in range(8):
                    p.op("pe", "transpose", out=psb[0][:, c * 128:(c + 1) * 128], in_=mixed[:, t, c * 128:(c + 1) * 128],
                         identity=ident[:], reads=[("mx", t), "ident"], writes=[("psb", 0)])
                if t % 2 == 0:
                    p.op("act", "activation", out=hT[:, :, t * 128:(t + 1) * 128],
                         in_=psb[0][:].rearrange("p (c n) -> p c n", n=128), func=AF.Copy,
                         reads=[("psb", 0)], writes=[("hT", t)])
                else:
                    p.op("dve", "tensor_copy", out=hT[:, :, t * 128:(t + 1) * 128],
                         in_=psb[0][:].rearrange("p (c n) -> p c n", n=128),
                         reads=[("psb", 0)], writes=[("hT", t)])

            def stM(t):
                for half in range(2):
                    bank = (2 * t + half) % 4
                    for c in range(8):
                        p.op("pe", "matmul", ps[bank][:, :], hT[:, c, t * 128:(t + 1) * 128],
                             wout_v[:, c, half * 512:(half + 1) * 512], start=(c == 0), stop=(c == 7),
                             reads=[("w", c // 4), ("hT", t)], writes=[("ps", bank)], **MMK)
                    p.op("dve", "tensor_tensor", out=xres[:, t, half * 512:(half + 1) * 512], in0=ps[bank][:, :],
                         in1=xres[:, t, half * 512:(half + 1) * 512], op=ALU.add,
                         reads=[("ps", bank), ("x", t)], writes=[("x", t)])
                if nmode != "none":
                    p.op("act", "activation", out=junkb, in_=xres[:, t, :], func=AF.Square, accum_out=ssq[:, t:t + 1],
                         reads=[("x", t)], writes=[("ssq", t), ("e", 0), ("e", 1)])
                    p.op("act", "activation", out=lnv[:, t:t + 1], in_=ssq[:, t:t + 1], func=AF.Ln, scale=1.0 / D,
                         bias=epsb[:], reads=[("ssq", t), "epsb"], writes=[("lnv_t", t)])
                    p.op("act", "activation", out=rstd[:, t:t + 1], in_=lnv[:, t:t + 1], func=AF.Exp, scale=-0.5,
                         reads=[("lnv_t", t)], writes=[("rstd_t", t)])

            def stN_dve(t):
                if nmode == "layer":
                    i = t % 2
                    p.op("dve", "scalar_tensor_tensor", out=hb[i][:], in0=xres[:, t, :], scalar=rstd[:, t:t + 1],
                         in1=gA[:], op0=ALU.mult, op1=ALU.mult,
                         reads=[("x", t), ("rstd_t", t), "gA"], writes=[("hb", i)])
                elif nmode == "final":
                    p.op("dve", "scalar_tensor_tensor", out=xres[:, t, :], in0=xres[:, t, :], scalar=rstd[:, t:t + 1],
                         in1=gA[:], op0=ALU.mult, op1=ALU.mult, reads=[("x", t), ("rstd_t", t), "gA"],
                         writes=[("x", t)])

            def stN(t):
                if nmode == "layer":
                    i = t % 2
                    for c in range(8):
                        p.op("pe", "transpose", out=psb[1][:, c * 128:(c + 1) * 128], in_=hb[i][:, c * 128:(c + 1) * 128],
                             identity=ident[:], reads=[("hb", i), "ident"], writes=[("psb", 1)])
                    p.op("act", "activation", out=hT[:, :, t * 128:(t + 1) * 128],
                         in_=psb[1][:].rearrange("p (c n) -> p c n", n=128), func=AF.Copy,
                         reads=[("psb", 1)], writes=[("hT", t)])
                elif nmode == "final":
                    if t % 4 == 3:
                        t0 = t - 3
                        dst = out_d[t0 * 128:(t0 + 4) * 128, :].rearrange("(t p) d -> p t d", p=128)
                        p.dma("out", dst, xres[:, t0:t0 + 4, :], reads=[("x", tt) for tt in range(t0, t0 + 4)])

            for it in range(NT + 3):
                if 0 <= it - 3 < NT:
                    stN_dve(it - 3)
                if it < NT:
                    stT(it)
                if 0 <= it - 1 < NT:
                    stM(it - 1)
                if 0 <= it - 3 < NT:
                    stN(it - 3)
            p.alias_transfer([("lnv_t", t) for t in range(NT)] + [("rstd_t", t) for t in range(NT)], ["lnv", "rstd"])
            if nmode == "layer":
                fused_norm_done = True
            if nmode == "final":
                final_done = True
            if "x1" in dbg and li == int(os.environ.get("DBGLAYER", "0")):
                for t0 in range(0, NT, 4):
                    p.dma("dbg", dbg_d["x1"][t0 * 128:(t0 + 4) * 128, :].rearrange("(t p) d -> p t d", p=128),
                          xres[:, t0:t0 + 4, :], reads=[("x", t) for t in range(t0, t0 + 4)])

        if "mixed" in dbg and "mixed" not in dumped:
            for t0 in range(0, NT, 4):
                p.dma("dbg", dbg_d["mixed"][t0 * 128:(t0 + 4) * 128, :].rearrange("(t p) d -> p t d", p=128),
                      mixed[:, t0:t0 + 4, :], reads=[("mx", t) for t in range(t0, t0 + 4)])
        if final_done:
            pass
        elif do_final and (stop in (None, "normfinal") or "FINAL" in os.environ):
            p.dma("gA", gA[:], fg_d[0:1, :].partition_broadcast(128), writes=["gA"])
            rmsnorm_stats(lambda t: xres[:, t, :], lambda t: [("x", t)], NT)
            for t in range(NT):
                p.op("dve", "scalar_tensor_tensor", out=xres[:, t, :], in0=xres[:, t, :], scalar=rstd[:, t:t + 1],
                     in1=gA[:], op0=ALU.mult, op1=ALU.mult, reads=[("x", t), "rstd", "gA"], writes=[("x", t)])
        if not final_done:
            for t0 in range(0, NT, 4):
                dst = out_d[t0 * 128:(t0 + 4) * 128, :].rearrange("(t p) d -> p t d", p=128)
                p.dma("out", dst, xres[:, t0:t0 + 4, :], reads=[("x", t) for t in range(t0, t0 + 4)])
        p.wait_all_dma("out")
        if "out2" in dbg:
            o2 = dram("dbg_out2", [T, D], F32, kind="ExternalOutput")
            for t0 in range(0, NT, 4):
                p.dma("dbg2", o2[t0 * 128:(t0 + 4) * 128, :].rearrange("(t p) d -> p t d", p=128),
                      xres[:, t0:t0 + 4, :], reads=[("x", t) for t in range(NT)] + ["rstd"])
            p.wait_all_dma("dbg2")
        if "dbg" in p.chan_n:
            p.wait_all_dma("dbg")
        p.emit()
        build_program.stats = p.stats
    return nc


def host_constants():
    bf = ml_dtypes.bfloat16
    c = {}
    c["c_ident"] = np.eye(128, dtype=np.float32).astype(bf)
    j = np.arange(128)[:, None]
    s = np.arange(128)[None, :]
    c["c_negtri"] = np.where(j >= s, -1.0, 0.0).astype(np.float32).astype(bf)
    c["c_negones"] = np.full((128, 128), -1.0, np.float32).astype(bf)
    c["c_dmask_sb"] = np.where(j >= s, NEG, 0.0).astype(np.float32).astype(bf)
    c["c_dmask_mb"] = np.where(j > s, NEG, 0.0).astype(np.float32).astype(bf)
    kind = np.zeros((8, T), np.float32)
    for m in range(8):
        kind[m, m * 256:(m + 1) * 256] = 1.0
    c["c_kind"] = kind.astype(bf)
    pos = (np.arange(NT)[None, :] * 128 + np.arange(128)[:, None]).astype(np.float64)
    inv = 500000.0 ** (-np.arange(8, dtype=np.float64) * 2.0 / 16.0)
    ang = pos[:, :, None] * inv[None, None, :]
    cc = np.concatenate([np.cos(ang), np.cos(ang)], axis=2)
    ss = np.concatenate([np.sin(ang), np.sin(ang)], axis=2)
    c["c_cc"] = cc.reshape(128, NT * 16).astype(np.float32)
    c["c_ss"] = ss.reshape(128, NT * 16).astype(np.float32)
    own = np.arange(NT) // 2
    nb = np.arange(8)
    gb = np.where(nb[None, :] < own[:, None], 0.0, -1e30).astype(np.float32)
    mt = np.where(nb[None, :] < own[:, None], -NEG, 0.0).astype(np.float32)
    c["c_gbias"] = np.broadcast_to(gb[None], (128, NT, 8)).reshape(128, NT * 8).copy()
    c["c_mtile"] = np.broadcast_to(mt[None], (128, NT, 8)).reshape(128, NT * 8).copy()
    return c


def relayout_w_in(w_in):
    Lw = w_in.shape[0]
    out = np.zeros((Lw, 8, D, 512), np.float32)
    for pr in range(3):
        for k in range(4):
            out[:, pr, :, k * 128:(k + 1) * 128] = w_in[:, :, k * 384 + pr * 128: k * 384 + (pr + 1) * 128]
            out[:, 3 + pr, :, k * 128:(k + 1) * 128] = w_in[:, :, 1536 + k * 384 + pr * 128: 1536 + k * 384 + (pr + 1) * 128]
    for pp in range(2):
        out[:, 6 + pp, :, 0:128] = w_in[:, :, 3072 + pp * 128:3072 + (pp + 1) * 128]
        out[:, 6 + pp, :, 128:256] = w_in[:, :, 3328 + pp * 128:3328 + (pp + 1) * 128]
    out = out.reshape(Lw, 8, 8, 128, 512).transpose(0, 1, 3, 2, 4)
    return np.ascontiguousarray(out).reshape(Lw * 8 * 128, 8 * 512)


def pmajor(w):
    Lw, _, n = w.shape
    return np.ascontiguousarray(w.reshape(Lw, 8, 128, n).transpose(0, 2, 1, 3)).reshape(Lw * 128, 8 * n)


_CACHE = {}


def kernel(x, mem, norm_g, w_in, mem_norm_g, w_mem_kv, w_out, final_norm_g):
    x = np.asarray(x, np.float32)
    mem = np.asarray(mem, np.float32)
    consts = host_constants()
    shared = dict(consts)
    shared["w_in_r"] = relayout_w_in(np.asarray(w_in, np.float32))
    shared["w_mem_kv"] = pmajor(np.asarray(w_mem_kv, np.float32))
    shared["w_out"] = pmajor(np.asarray(w_out, np.float32))
    shared["norm_g"] = np.ascontiguousarray(np.asarray(norm_g, np.float32))
    shared["mem_norm_g"] = np.ascontiguousarray(np.asarray(mem_norm_g, np.float32))
    shared["final_norm_g"] = np.ascontiguousarray(np.asarray(final_norm_g, np.float32).reshape(1, D))
    if "nc" not in _CACHE:
        _CACHE["nc"] = build_program()
    nc = _CACHE["nc"]
    in_maps = []
    for b in range(8):
        m = dict(shared)
        m["x"] = np.ascontiguousarray(x[b])
        m["mem"] = np.ascontiguousarray(mem[b])
        in_maps.append(m)
    res = run_bass_kernel_spmd(nc, in_maps, core_ids=list(range(8)))
    return np.stack([np.asarray(r["out"], np.float32) for r in res.results], axis=0)
```

```python
import os
import numpy as np
import ml_dtypes
from contextlib import ExitStack

import concourse.bass as bass
import concourse.mybir as mybir
from concourse.bass_utils import run_bass_kernel_spmd

F32 = mybir.dt.float32
BF16 = mybir.dt.bfloat16
AF = mybir.ActivationFunctionType
ALU = mybir.AluOpType
AX = mybir.AxisListType

T = 2048
D = 1024
L = 2
NT = 16
MEM = 256
NEG = -30000.0
EPS = 1e-6
STRICT = os.environ.get("KSTRICT", "0") == "1"
SIDE_MID = os.environ.get("KSIDEMID", "1") == "1"
QK_DVE = os.environ.get("KQKDVE", "1") == "1"


class Prog:
    ENG = ("pe", "act", "dve", "pool", "sp")

    def __init__(self, nc):
        self.nc = nc
        self.ins = {e: [] for e in self.ENG}
        self.res = {}
        self.chan_n = {}

    def _deps(self, me, eng, reads, writes):
        deps = set()
        raw = set()
        rr = set()
        for r in reads:
            st = self.res.setdefault(r, [None, []])
            if st[0] is not None:
                deps.add(st[0])
                raw.add(st[0])
            if isinstance(r, tuple) and r[0] in ("ps", "psb"):
                for rd in st[1]:
                    if not (rd[0] == "e" and rd[1] == eng):
                        rr.add(rd)
        for w in writes:
            st = self.res.setdefault(w, [None, []])
            if st[0] is not None:
                deps.add(st[0])
            for rd in st[1]:
                deps.add(rd)
        for r in reads:
            self.res[r][1].append(me)
        for w in writes:
            self.res[w] = [me, []]
        out = []
        for d in deps | rr:
            if d == me:
                continue
            if d[0] == "e" and d[1] == eng:
                if eng == "pe" or (d not in raw and not STRICT):
                    continue
            out.append(d)
        return out

    def alias_transfer(self, from_list, to_list):
        acc = []
        for r in from_list:
            st = self.res.get(r)
            if st is None:
                continue
            if st[0] is not None:
                acc.append(st[0])
            acc.extend(st[1])
        for r in to_list:
            st = self.res.setdefault(r, [None, []])
            st[1] = list(st[1]) + acc

    def op(self, eng, meth, *args, reads=(), writes=(), **kw):
        idx = len(self.ins[eng])
        me = ("e", eng, idx)
        deps = self._deps(me, eng, reads, writes)
        self.ins[eng].append(dict(meth=meth, args=args, kw=kw, deps=deps, sig=False, chan=None))
        return me

    def dma(self, chan, out, in_, reads=(), writes=(), eng="sp", **kw):
        k = self.chan_n.get(chan, 0) + 1
        self.chan_n[chan] = k
        me = ("d", chan, k)
        deps = self._deps(me, eng, reads, writes)
        self.ins[eng].append(dict(meth="dma_start", args=(), kw=dict(out=out, in_=in_, **kw), deps=deps,
                                  sig=False, chan=chan))
        return me

    def wait_all_dma(self, chan, eng="sp"):
        k = self.chan_n[chan]
        self.ins[eng].append(dict(meth=None, args=(), kw={}, deps=[("d", chan, k)], sig=False, chan=None))

    def emit(self):
        nc = self.nc
        for e in self.ENG:
            for ins in self.ins[e]:
                for d in ins["deps"]:
                    if d[0] == "e":
                        self.ins[d[1]][d[2]]["sig"] = True
        cnt = {}
        for e in self.ENG:
            c = 0
            arr = []
            for ins in self.ins[e]:
                if ins["sig"]:
                    c += 1
                arr.append(c)
            cnt[e] = arr
        with ExitStack() as es:
            sem_e = {e: es.enter_context(nc.semaphore("sem_" + e)) for e in self.ENG}
            sem_c = {c: es.enter_context(nc.semaphore("dma_" + str(c))) for c in self.chan_n}
            block = es.enter_context(nc.Block())
            stats = {}

            def replay(e, engobj):
                waited = {}
                nw = 0
                for ins in self.ins[e]:
                    need = {}
                    for d in ins["deps"]:
                        if d[0] == "e":
                            key = ("e", d[1])
                            val = cnt[d[1]][d[2]]
                        else:
                            key = ("d", d[1])
                            val = 16 * d[2]
                        if val > need.get(key, 0):
                            need[key] = val
                    for key, val in need.items():
                        if waited.get(key, 0) >= val:
                            continue
                        waited[key] = val
                        sem = sem_e[key[1]] if key[0] == "e" else sem_c[key[1]]
                        engobj.wait_ge(sem, val)
                        nw += 1
                    if ins["meth"] is None:
                        continue
                    inst = getattr(engobj, ins["meth"])(*ins["args"], **ins["kw"])
                    if ins["chan"] is not None:
                        inst.then_inc(sem_c[ins["chan"]], 16)
                    elif ins["sig"]:
                        inst.then_inc(sem_e[e], 1)
                stats[e] = (len(self.ins[e]), nw)

            @block.tensor
            def _(eng):
                replay("pe", eng)

            @block.scalar
            def _(eng):
                replay("act", eng)

            @block.vector
            def _(eng):
                replay("dve", eng)

            @block.gpsimd
            def _(eng):
                replay("pool", eng)

            @block.sync
            def _(eng):
                replay("sp", eng)
        self.stats = stats


def build_program(layers=(0, 1), do_final=True, dbg=(), stop=None):
    nc = bass.Bass("TRN2", target_bir_lowering=False)

    def dram(name, shape, dt, kind="ExternalInput"):
        return nc.dram_tensor(name, list(shape), dt, kind=kind).ap()

    x_d = dram("x", [T, D], F32)
    mem_d = dram("mem", [MEM, D], F32)
    win_d = dram("w_in_r", [L * 8 * 128, 8 * 512], F32)
    wkv_d = dram("w_mem_kv", [L * 128, 8 * 512], F32)
    wout_d = dram("w_out", [L * 128, 8 * D], F32)
    ng_d = dram("norm_g", [L, D], F32)
    mg_d = dram("mem_norm_g", [L, D], F32)
    fg_d = dram("final_norm_g", [1, D], F32)
    ident_d = dram("c_ident", [128, 128], BF16)
    negtri_d = dram("c_negtri", [128, 128], BF16)
    negones_d = dram("c_negones", [128, 128], BF16)
    dmsb_d = dram("c_dmask_sb", [128, 128], BF16)
    dmmb_d = dram("c_dmask_mb", [128, 128], BF16)
    kind_d = dram("c_kind", [8, T], BF16)
    cc_d = dram("c_cc", [128, NT * 16], F32)
    ss_d = dram("c_ss", [128, NT * 16], F32)
    gbias_d = dram("c_gbias", [128, NT * 8], F32)
    mtile_d = dram("c_mtile", [128, NT * 8], F32)
    out_d = dram("out", [T, D], F32, kind="ExternalOutput")
    dbg_d = {}
    if "mixed" in dbg:
        dbg_d["mixed"] = dram("dbg_mixed", [T, D], BF16, kind="ExternalOutput")
    if "x1" in dbg:
        dbg_d["x1"] = dram("dbg_x1", [T, D], F32, kind="ExternalOutput")
    if "hT" in dbg:
        dbg_d["hT"] = dram("dbg_hT", [128, 8 * T], BF16, kind="ExternalOutput")

    with ExitStack() as es:
        def sb(name, shape, dt):
            return es.enter_context(nc.sbuf_tensor(name, list(shape), dt))

        xres = sb("xres", [128, NT, D], F32)
        hT = sb("hT", [128, 8, T], BF16)
        mixed = sb("mixed", [128, NT, D], BF16)
        A = [sb("A0", [128, T], BF16), sb("A1", [128, T], BF16)]
        B = [sb("B0", [128, T], BF16), sb("B1", [128, T], BF16)]
        vaug = sb("vaug", [128, NT, 2, 66], BF16)
        sg = sb("sg", [128, NT, 128], BF16)
        wbuf = sb("wbuf", [128, 2, 8, 512], BF16)
        HB = sb("HB", [128, 2, D], BF16)
        hb = [HB[:, 0, :], HB[:, 1, :]]
        vaug2 = HB[:].rearrange("p a (t d) -> p (a t) d", d=64).rearrange("p (t h) d -> p t h d", h=2)
        ework = sb("ework", [128, D], F32)
        ebuf = [ework[:, 0:512], ework[:, 512:1024]]
        spb = [sb("sp0", [128, 512], BF16), sb("sp1", [128, 512], BF16)]
        sacc = [sb("sacc0", [128, 512], BF16), sb("sacc1", [128, 512], BF16)]
        wbf = [sb("wbf0", [128, 512], BF16), sb("wbf1", [128, 512], BF16)]
        ident = sb("ident", [128, 128], BF16)
        negtri = sb("negtri", [128, 128], BF16)
        negones = sb("negones", [128, 128], BF16)
        dmask_sb = sb("dmask_sb", [128, 128], BF16)
        dmask_mb = sb("dmask_mb", [128, 128], BF16)
        CC = sb("CC", [128, NT, 16], F32)
        SS = sb("SS", [128, NT, 16], F32)
        gbias = sb("gbias", [128, NT, 8], F32)
        mtile = sb("mtile", [128, NT, 8], F32)
        gA = sb("gA", [128, D], F32)
        ssq = sb("ssq", [128, NT], F32)
        lnv = sb("lnv", [128, NT], F32)
        rstd = sb("rstd", [128, NT], F32)
        epsb = sb("epsb", [128, 1], F32)
        RQV = sb("RQV", [128, NT * 2 * 72], BF16)
        Rq = RQV[:].rearrange("p (t h d) -> p t h d", t=NT, h=2)
        Rk = [sb("Rk%d" % i, [128, 2, 64], BF16) for i in range(3)]
        Ub = [sb("U0", [128, 4, 16], F32), sb("U1", [128, 4, 16], F32)]
        Vb = [sb("V0", [128, 4, 16], F32), sb("V1", [128, 4, 16], F32)]
        qkraw = [sb("qkraw0", [128, 4, 64], F32), sb("qkraw1", [128, 4, 64], F32)]
        ksum = sb("ksum", [64, 2, 8], F32)
        ksum_bf = sb("ksum_bf", [64, 2, 8], BF16)
        gm = sb("gm", [128, 2, NT, 8], F32)
        mx = sb("mx", [128, NT, 8], F32)
        sel = sb("sel", [128, NT, 8], F32)
        MTS = sb("MTS", [128, 8 * MEM], BF16)
        mT = MTS[:].rearrange("p (c m) -> p c m", c=8)
        sg2 = MTS[:].rearrange("p (t n) -> p t n", t=NT)
        mkT = sb("mkT", [128, 2, MEM], BF16)
        mvaug = sb("mvaug", [128, 2, 4, 66], BF16)
        rden = [sb("rden0", [128, 4, 1], F32), sb("rden1", [128, 4, 1], F32)]
        ps = [es.enter_context(nc.psum_tensor("ps%d" % i, [128, 512], F32)) for i in range(6)]
        psb = [es.enter_context(nc.psum_tensor("psb%d" % i, [128, 1024], BF16)) for i in range(2)]

        p = Prog(nc)
        MMK = dict(skip_group_check=True)

        for t0 in range(0, NT, 4):
            src = x_d[t0 * 128:(t0 + 4) * 128, :].rearrange("(t p) d -> p t d", p=128)
            p.dma("x%d" % t0, xres[:, t0:t0 + 4, :], src, writes=[("x", t) for t in range(t0, t0 + 4)])
        p.dma("c_ident", ident[:], ident_d, writes=["ident"])
        p.dma("c_negtri", negtri[:], negtri_d, writes=["negtri"])
        p.dma("c_negones", negones[:], negones_d, writes=["negones"])
        p.dma("c_dmsb", dmask_sb[:], dmsb_d, writes=["dmask_sb"])
        p.dma("c_dmmb", dmask_mb[:], dmmb_d, writes=["dmask_mb"])
        p.dma("c_cc", CC[:].rearrange("p t i -> p (t i)"), cc_d, writes=["CC"])
        p.dma("c_ss", SS[:].rearrange("p t i -> p (t i)"), ss_d, writes=["SS"])
        p.dma("c_gbias", gbias[:].rearrange("p t i -> p (t i)"), gbias_d, writes=["gbias"])
        p.dma("c_mtile", mtile[:].rearrange("p t i -> p (t i)"), mtile_d, writes=["mtile"])
        p.op("pool", "memset", epsb[:], EPS, writes=["epsb"])
        import os
        if "MEMSETMIXED" in os.environ:
            p.op("pool", "memset", mixed[:], 0.0, writes=[("mx", t) for t in range(NT)])
        p.op("pool", "memset", vaug[:, :, :, 64:65], 1.0, writes=[("vone",)])
        p.op("pool", "memset", mvaug[:, :, :, 64:65], 1.0, writes=[("mvone",)])

        wq = {"n": 0}

        def load_w(src_rows, ncols=512, slot=None):
            if slot is None:
                slot = wq["n"] % 2
                wq["n"] += 1
            p.dma("w%d" % slot, wbuf[:, slot].rearrange("p c n -> p (c n)"), src_rows, eng="pool",
                  writes=[("w", slot)])
            return slot

        def rmsnorm_stats(src_fn, res_fn, n):
            for t in range(n):
                p.op("dve", "scalar_tensor_tensor", out=hb[1][:], in0=src_fn(t), scalar=1.0, in1=src_fn(t),
                     op0=ALU.mult, op1=ALU.mult, accum_out=ssq[:, t:t + 1],
                     reads=res_fn(t), writes=[("ssq", t), ("hb", 1)])
            p.op("act", "activation", out=lnv[:, 0:n], in_=ssq[:, 0:n], func=AF.Ln, scale=1.0 / D, bias=epsb[:],
                 reads=[("ssq", t) for t in range(n)] + ["epsb"], writes=["lnv"])
            p.op("act", "activation", out=rstd[:, 0:n], in_=lnv[:, 0:n], func=AF.Exp, scale=-0.5,
                 reads=["lnv"], writes=["rstd"])

        def norm_transpose(src_fn, res_fn, n, dstT, dst_res_fn, dst_t0=0):
            rmsnorm_stats(src_fn, res_fn, n)
            for t in range(n):
                i = t % 2
                dt_ = dst_t0 + t
                p.op("dve", "scalar_tensor_tensor", out=hb[i][:], in0=src_fn(t), scalar=rstd[:, t:t + 1],
                     in1=gA[:], op0=ALU.mult, op1=ALU.mult,
                     reads=res_fn(t) + ["rstd", "gA"], writes=[("hb", i)])
                for c in range(8):
                    p.op("pe", "transpose", out=psb[i][:, c * 128:(c + 1) * 128], in_=hb[i][:, c * 128:(c + 1) * 128],
                         identity=ident[:], reads=[("hb", i), "ident"], writes=[("psb", i)])
                p.op("act", "activation", out=dstT[:, :, dt_ * 128:(dt_ + 1) * 128],
                     in_=psb[i][:].rearrange("p (c n) -> p c n", n=128), func=AF.Copy,
                     reads=[("psb", i)], writes=[dst_res_fn(dt_)])

        obank = {"n": 0}

        def attn_head(*a, **k):
            for _ in attn_head_gen(*a, **k):
                pass

        def attn_head_gen(kind, causal, q_ap, q_res, k_ap, k_res, v_ap, v_res, nkeys_tiles, dmask, dmask_res,
                          c_off, hl, exp_scale, sgt=None, sg_res=None):
            yield from attn_stream_gen(kind, [dict(causal=causal, q_ap=q_ap, q_res=q_res, k_ap=k_ap, k_res=k_res,
                                                   v_ap=v_ap, v_res=v_res, nkeys_tiles=nkeys_tiles, dmask=dmask,
                                                   dmask_res=dmask_res, c_off=c_off, hl=hl, exp_scale=exp_scale,
                                                   sgt=sgt, sg_res=sg_res)])

        def attn_stream_gen(kind, heads, side=None, side_every=1, side_burst=1):
            recs = []
            for H in heads:
                if H.get("sgt") is None:
                    H["sgt"] = sg
                    H["sg_res"] = lambda t: ("sg", 0, t)
                for G in range(4):
                    if H["causal"]:
                        kts = list(range(4 * G + 3, -1, -1)) if kind == "sb" else list(range(0, 4 * G + 4))
                    else:
                        kts = list(range(H["nkeys_tiles"]))
                    ob = 4 + (obank["n"] % 2)
                    obank["n"] += 1
                    grp = dict(H=H, G=G, ob=ob, first_pv=True)
                    for i, kt in enumerate(kts):
                        if H["causal"]:
                            j = max(kt - 4 * G, 0)
                            c0, dg = j * 128, kt >= 4 * G
                        else:
                            c0, dg = 0, False
                        recs.append(dict(grp=grp, i=i, kt=kt, c0=c0, dg=dg, last=(i == len(kts) - 1)))
            F = len(recs)

            def stageA(f):
                r = recs[f]; H = r["grp"]["H"]; G = r["grp"]["G"]
                kt, c0, dg = r["kt"], r["c0"], r["dg"]
                bank = f % 2
                p.op("pe", "matmul", ps[bank][:, c0:512], H["k_ap"](kt), H["q_ap"](G, c0), start=True, stop=not dg,
                     reads=[H["k_res"](kt), H["q_res"](G)], writes=[("ps", bank)], **MMK)
                if dg:
                    p.op("pe", "matmul", ps[bank][:, c0:c0 + 128], ident[:], H["dmask"][:], start=False, stop=True,
                         reads=["ident", H["dmask_res"]], writes=[("ps", bank)], **MMK)

            def stageB1(f):
                r = recs[f]; H = r["grp"]["H"]
                c0 = r["c0"]
                bank = f % 2
                if kind == "sb":
                    p.op("act", "activation", out=ebuf[bank][:, c0:512], in_=ps[bank][:, c0:512], func=AF.Exp,
                         reads=[("ps", bank)], writes=[("e", bank)])
                else:
                    p.op("act", "activation", out=wbf[bank][:, c0:512], in_=ps[bank][:, c0:512], func=AF.Exp,
                         scale=H["exp_scale"], reads=[("ps", bank)], writes=[("wbf", bank)])

            def stageB2(f):
                c0 = recs[f]["c0"]
                bank = f % 2
                p.op("act", "activation", out=spb[bank][:, c0:512], in_=ebuf[bank][:, c0:512], func=AF.Ln,
                     bias=1.0, reads=[("e", bank)], writes=[("sp", bank)])

            def sacc_update(f):
                pc0 = recs[f - 1]["c0"]
                bank = f % 2
                pb = (f - 1) % 2
                p.op("dve", "tensor_tensor", out=sacc[bank][:, pc0:512], in0=sacc[pb][:, pc0:512],
                     in1=spb[pb][:, pc0:512], op=ALU.add,
                     reads=[("sacc", pb), ("sp", pb)], writes=[("sacc", bank)])

            def stageC(f):
                r = recs[f]; H = r["grp"]["H"]; G = r["grp"]["G"]
                kt, c0, dg, i = r["kt"], r["c0"], r["dg"], r["i"]
                bank = f % 2
                pb = 2 + bank
                p.op("pe", "matmul", ps[pb][:, c0:512], H["k_ap"](kt), H["q_ap"](G, c0), start=True, stop=False,
                     reads=[H["k_res"](kt), H["q_res"](G)], writes=[("ps", pb)], **MMK)
                last = (i == 0) and (not dg)
                p.op("pe", "matmul", ps[pb][:, c0:512], negtri[:], spb[bank][:, c0:512], start=False, stop=last,
                     reads=["negtri", ("sp", bank)], writes=[("ps", pb)], **MMK)
                if i >= 1:
                    p.op("pe", "matmul", ps[pb][:, c0:512], negones[:], sacc[bank][:, c0:512], start=False,
                         stop=not dg, reads=["negones", ("sacc", bank)], writes=[("ps", pb)], **MMK)
                if dg:
                    p.op("pe", "matmul", ps[pb][:, c0:c0 + 128], ident[:], H["dmask"][:], start=False, stop=True,
                         reads=["ident", H["dmask_res"]], writes=[("ps", pb)], **MMK)

            def stageD(f):
                c0 = recs[f]["c0"]
                bank = f % 2
                p.op("act", "activation", out=wbf[bank][:, c0:512], in_=ps[2 + bank][:, c0:512], func=AF.Exp,
                     reads=[("ps", 2 + bank)], writes=[("wbf", bank)])

            def stageE(f):
                r = recs[f]; grp = r["grp"]; H = grp["H"]
                kt, c0 = r["kt"], r["c0"]
                bank = f % 2
                ob = grp["ob"]
                nv = 64 if kind == "sb" else 65
                for ql in range(c0 // 128, 4):
                    p.op("pe", "matmul", ps[ob][:, ql * 66:ql * 66 + nv], wbf[bank][:, ql * 128:(ql + 1) * 128],
                         H["v_ap"](kt), start=grp["first_pv"], stop=False,
                         reads=[("wbf", bank), H["v_res"](kt)], writes=[("ps", ob)], **MMK)
                    grp["first_pv"] = False
                if r["last"]:
                    evac(grp)

            def evac(grp):
                H = grp["H"]; G = grp["G"]; ob = grp["ob"]
                sgt, sg_res, c_off, hl = H["sgt"], H["sg_res"], H["c_off"], H["hl"]
                rd = rden[ob - 4]
                pov = ps[ob][:, 0:264].rearrange("p (a b) -> p a b", b=66)
                if kind == "sb":
                    p.op("dve", "tensor_tensor", out=mixed[:, 4 * G:4 * G + 4, c_off:c_off + 64], in0=pov[:, :, 0:64],
                         in1=sgt[:, 4 * G:4 * G + 4, hl * 64:(hl + 1) * 64], op=ALU.mult,
                         reads=[("ps", ob)] + [sg_res(t) for t in range(4 * G, 4 * G + 4)],
                         writes=[("mx", t) for t in range(4 * G, 4 * G + 4)])
                else:
                    p.op("dve", "reciprocal", out=rd[:], in_=pov[:, :, 64:65], reads=[("ps", ob)],
                         writes=[("rden", ob)])
                    for ql in range(4):
                        t = 4 * G + ql
                        p.op("dve", "scalar_tensor_tensor", out=mixed[:, t, c_off:c_off + 64],
                             in0=ps[ob][:, ql * 66:ql * 66 + 64], scalar=rd[:, ql, :],
                             in1=sgt[:, t, hl * 64:(hl + 1) * 64], op0=ALU.mult, op1=ALU.mult,
                             reads=[("ps", ob), ("rden", ob), sg_res(t)], writes=[("mx", t)])

            def side_step(it=0):
                nonlocal side
                if side is not None and it % side_every == side_every - 1:
                    try:
                        for _ in range(side_burst):
                            next(side)
                    except StopIteration:
                        side = None

            if kind == "sb":
                stageA(0)
                stageB1(0)
                for it in range(F + 2):
                    if it + 1 < F:
                        stageA(it + 1)
                    if it < F:
                        stageB2(it)
                    if 0 <= it - 1 < F:
                        stageC(it - 1)
                    if it + 1 < F and recs[it + 1]["i"] >= 1:
                        if recs[it + 1]["i"] == 1:
                            p.op("pool", "memset", sacc[0][:], 0.0, writes=[("sacc", 0)])
                            p.op("pool", "memset", sacc[1][:], 0.0, writes=[("sacc", 1)])
                        sacc_update(it + 1)
                    if it + 1 < F:
                        stageB1(it + 1)
                    if 0 <= it - 1 < F:
                        stageD(it - 1)
                    if SIDE_MID:
                        side_step(it)
                    if 0 <= it - 2 < F:
                        stageE(it - 2)
                    if not SIDE_MID:
                        side_step(it)
                    yield
            else:
                for it in range(F + 1):
                    if it < F:
                        stageA(it)
                        stageB1(it)
                    side_step(it)
                    if 0 <= it - 1 < F:
                        stageE(it - 1)
                    yield
            if side is not None:
                for _ in side:
                    pass

        dumped = set()
        fused_norm_done = False
        final_done = False
        for li, l in enumerate(layers):
            wrow = lambda g: win_d[(l * 8 + g) * 128:(l * 8 + g + 1) * 128, :]
            if not fused_norm_done:
                p.dma("gA", gA[:], ng_d[l:l + 1, :].partition_broadcast(128), writes=["gA"])
            slot_next = load_w(wrow(0))
            if not fused_norm_done:
                norm_transpose(lambda t: xres[:, t, :], lambda t: [("x", t)], NT, hT, lambda t: ("hT", t))
            fused_norm_done = False
            if "hT" in dbg and li == 0:
                p.dma("dbg", dbg_d["hT"], hT[:].rearrange("p c t -> p (c t)"), reads=[("hT", t) for t in range(NT)])

            if stop in ("norm", "normfinal"):
                break
            VS = [vaug, vaug2]
            SGS = [sg, sg2]
            p.alias_transfer([("hb", 0), ("hb", 1)], [("v", 1, t) for t in range(NT)])
            p.alias_transfer([("mT", 0), ("mT", 1)], [("sg", 1, t) for t in range(NT)])

            def sb_proj_gen(pr, slot):
                st = (pr + 1) % 2
                k = 0
                for tg in range(4):
                    for which in range(2):
                        bi = k % 2
                        k += 1
                        pbank = psb[bi][:].bitcast(F32)
                        for c in range(8):
                            p.op("pe", "matmul", pbank, wbuf[:, slot, c, which * 128:(which + 1) * 128],
                                 hT[:, c, tg * 512:(tg + 1) * 512], start=(c == 0), stop=(c == 7),
                                 reads=[("w", slot)] + [("hT", t) for t in range(4 * tg, 4 * tg + 4)],
                                 writes=[("psb", bi)], **MMK)
                            if c == 3:
                                yield
                        if which == 0:
                            p.op("dve", "tensor_scalar", out=A[st][:, tg * 512:(tg + 1) * 512], in0=pbank,
                                 scalar1=0.125, scalar2=None, op0=ALU.mult,
                                 reads=[("psb", bi)], writes=[("A", st, tg)])
                        else:
                            p.op("dve", "tensor_copy", out=B[st][:, tg * 512:(tg + 1) * 512], in_=pbank,
                                 reads=[("psb", bi)], writes=[("B", st, tg)])
                        yield
                for t in range(NT):
                    bi = k % 2
                    k += 1
                    pbank = psb[bi][:].bitcast(F32)
                    for c in range(8):
                        p.op("pe", "matmul", pbank[:, 0:256], hT[:, c, t * 128:(t + 1) * 128],
                             wbuf[:, slot, c, 256:512], start=(c == 0), stop=(c == 7),
                             reads=[("w", slot), ("hT", t)], writes=[("psb", bi)], **MMK)
                    p.op("dve", "tensor_copy", out=VS[st][:, t, :, 0:64],
                         in_=pbank[:, 0:128].rearrange("p (h d) -> p h d", d=64),
                         reads=[("psb", bi)], writes=[("v", st, t)])
                    p.op("dve", "tensor_copy", out=SGS[st][:, t, :], in_=pbank[:, 128:256],
                         reads=[("psb", bi)], writes=[("sg", st, t)])
                    yield
                sgflat = SGS[st][:].rearrange("p t n -> p (t n)") if st == 0 else MTS[:]
                p.op("act", "activation", out=sgflat, in_=sgflat, func=AF.Silu,
                     reads=[("sg", st, t) for t in range(NT)], writes=[("sg", st, t) for t in range(NT)])
                yield

            def sb_attn_gen(pr, side=None):
                st = (pr + 1) % 2
                heads = []
                for hl in range(2):
                    r0, r1 = hl * 64, (hl + 1) * 64
                    heads.append(dict(
                        causal=True,
                        q_ap=lambda G, c0, r0=r0, r1=r1: A[st][r0:r1, G * 512 + c0:(G + 1) * 512],
                        q_res=lambda G: ("A", st, G),
                        k_ap=lambda kt, r0=r0, r1=r1: B[st][r0:r1, kt * 128:(kt + 1) * 128],
                        k_res=lambda kt: ("B", st, kt // 4),
                        v_ap=lambda kt, hl=hl: VS[st][:, kt, hl, 0:64],
                        v_res=lambda kt: ("v", st, kt),
                        nkeys_tiles=None, dmask=dmask_sb, dmask_res="dmask_sb",
                        c_off=(2 * pr + hl) * 64, hl=hl, exp_scale=1.0,
                        sgt=SGS[st], sg_res=lambda t: ("sg", st, t)))
                yield from attn_stream_gen("sb", heads, side=side, side_every=int(os.environ.get("KSBE", "3")),
                                           side_burst=int(os.environ.get("KSBB", "1")))

            def interleave(main, side, every, first):
                i = 0
                side_live = side is not None
                for _ in main:
                    i += 1
                    if side_live and i >= first and (i - first) % every == 0:
                        try:
                            next(side)
                        except StopIteration:
                            side_live = False
                if side_live:
                    for _ in side:
                        pass

            def moba_setup(hl):
                p.op("pool", "memset", B[hl][64:96, :], 0.0, writes=[("B", hl, tg) for tg in range(4)])
                p.op("pool", "memset", A[hl][64:96, :], 0.0, writes=[("A", hl, tg) for tg in range(4)])
                p.dma("kind%d" % hl, B[hl][64:72, :], kind_d, writes=[("B", hl, tg) for tg in range(4)])

            mbs = {"slot_next": None, "slots": {}}

            def mb_proj_gen(u, early=False):
                pr, hl = u // 2, u % 2
                PF = psb[0][:].bitcast(F32)

                def mm_out(t):
                    return (PF[:, 0:256], ("psb", 0)) if early else (ps[2 + t % 2][:, 0:256], ("ps", 2 + t % 2))

                def tq(t):
                    col = (t % 4) * 128
                    return (psb[1][0:64, col:col + 128], ("psb", 1)) if early else (psb[0][0:64, col:col + 128], ("psb", 0))

                def tk(t):
                    col = (t % 4) * 128
                    return (psb[1][0:64, 512 + col:512 + col + 128], ("psb", 1)) if early else (psb[1][0:64, col:col + 128], ("psb", 1))
                if hl == 0:
                    mbs["slots"][pr] = mbs["slot_next"]
                    mbs["slot_next"] = load_w(wrow(3 + pr + 1)) if pr < 2 else load_w(wrow(6), 256)
                slot = mbs["slots"][pr]

                def wv(c):
                    return wbuf[:, slot, c, :].rearrange("p (k h d) -> p k h d", k=4, h=2)[:, :, hl, :]

                def stage1(t):
                    mo, mres = mm_out(t)
                    i = t % 2
                    j = t % 3
                    for c in range(8):
                        p.op("pe", "matmul", mo, hT[:, c, t * 128:(t + 1) * 128], wv(c),
                             start=(c == 0), stop=(c == 7), reads=[("w", slot), ("hT", t)],
                             writes=[mres], **MMK)
                        if c == 3:
                            yield
                    pv4 = mo.rearrange("p (a b) -> p a b", b=64)
                    if early or QK_DVE:
                        p.op("dve", "tensor_copy", out=qkraw[i][:, 0:2, :], in_=pv4[:, 0:2, :],
                             reads=[mres], writes=[("qkraw", i)])
                    else:
                        p.op("act", "activation", out=qkraw[i][:, 0:2, :], in_=pv4[:, 0:2, :], func=AF.Copy,
                             reads=[mres], writes=[("qkraw", i)])
                    p.op("dve", "tensor_copy", out=vaug[:, t, hl, 0:64], in_=mo[:, 128:192],
                         reads=[mres], writes=[("vh", t, hl)])
                    p.op("dve", "tensor_copy", out=sg[:, t, hl * 64:(hl + 1) * 64], in_=mo[:, 192:256],
                         reads=[mres], writes=[("sgh", t, hl)])
                    p.op("pool", "tensor_tensor", out=Ub[i][:, 0:2, :], in0=qkraw[i][:, 0:2, 0:16],
                         in1=CC[:, t:t + 1, :].to_broadcast([128, 2, 16]), op=ALU.mult,
                         reads=[("qkraw", i), "CC"], writes=[("U", i)])
                    p.op("pool", "tensor_tensor", out=Vb[i][:, 0:2, :], in0=qkraw[i][:, 0:2, 0:16],
                         in1=SS[:, t:t + 1, :].to_broadcast([128, 2, 16]), op=ALU.mult,
                         reads=[("qkraw", i), "SS"], writes=[("V", i)])
                    p.op("pool", "tensor_tensor", out=Rq[:, t, hl, 0:8], in0=Ub[i][:, 0, 0:8], in1=Vb[i][:, 0, 8:16],
                         op=ALU.subtract, reads=[("U", i), ("V", i)], writes=[("Rq_rope", t, hl)])
                    p.op("pool", "tensor_tensor", out=Rq[:, t, hl, 8:16], in0=Ub[i][:, 0, 8:16], in1=Vb[i][:, 0, 0:8],
                         op=ALU.add, reads=[("U", i), ("V", i)], writes=[("Rq_rope", t, hl)])
                    p.op("pool", "tensor_tensor", out=Rk[j][:, 0, 0:8], in0=Ub[i][:, 1, 0:8], in1=Vb[i][:, 1, 8:16],
                         op=ALU.subtract, reads=[("U", i), ("V", i)], writes=[("Rk_rope", j)])
                    p.op("pool", "tensor_tensor", out=Rk[j][:, 0, 8:16], in0=Ub[i][:, 1, 8:16], in1=Vb[i][:, 1, 0:8],
                         op=ALU.add, reads=[("U", i), ("V", i)], writes=[("Rk_rope", j)])
                    p.op("dve", "tensor_copy", out=Rq[:, t, hl, 16:64], in_=qkraw[i][:, 0, 16:64],
                         reads=[("qkraw", i)], writes=[("Rq_rest", t, hl)])
                    p.op("dve", "tensor_copy", out=Rk[j][:, 0, 16:64], in_=qkraw[i][:, 1, 16:64],
                         reads=[("qkraw", i)], writes=[("Rk_rest", j)])

                def stage2(t):
                    j = t % 3
                    tg = t // 4
                    qo, qres = tq(t)
                    ko, kres = tk(t)
                    p.op("pe", "transpose", out=qo, in_=Rq[:, t, hl, 0:64],
                         identity=ident[:], reads=[("Rq_rope", t, hl), ("Rq_rest", t, hl), "ident"],
                         writes=[qres])
                    p.op("pe", "transpose", out=ko, in_=Rk[j][:, 0, :],
                         identity=ident[:], reads=[("Rk_rope", j), ("Rk_rest", j), "ident"],
                         writes=[kres])
                    if t % 4 == 3:
                        qsrc = psb[1][0:64, 0:512] if early else psb[0][0:64, 0:512]
                        ksrc = psb[1][0:64, 512:1024] if early else psb[1][0:64, 0:512]
                        p.op("act", "activation", out=A[hl][0:64, tg * 512:(tg + 1) * 512],
                             in_=qsrc, func=AF.Copy,
                             reads=[qres], writes=[("A", hl, tg)])
                        p.op("dve", "tensor_copy", out=B[hl][0:64, tg * 512:(tg + 1) * 512],
                             in_=ksrc,
                             reads=[kres], writes=[("B", hl, tg)])

                for tt_ in range(NT + 2):
                    if tt_ < NT:
                        yield from stage1(tt_)
                    if tt_ - 2 >= 0:
                        stage2(tt_ - 2)
                    yield
                sgh = sg[:, :, hl * 64:(hl + 1) * 64]
                p.op("act", "activation", out=sgh, in_=sgh, func=AF.Silu,
                     reads=[("sgh", t, hl) for t in range(NT)], writes=[("sgh", t, hl) for t in range(NT)])
                yield
                p.op("dve", "tensor_reduce", out=ksum[:, hl, :],
                     in_=B[hl][0:64, :].rearrange("p (n k) -> p n k", k=256), axis=AX.X, op=ALU.add,
                     reads=[("B", hl, tg) for tg in range(4)], writes=[("ksum", hl)])
                p.op("dve", "tensor_copy", out=ksum_bf[:, hl, :], in_=ksum[:, hl, :],
                     reads=[("ksum", hl)], writes=[("ksum_bf", hl)])
                for t in range(NT):
                    col = t * 8
                    gdst = PF[:, 256 + col:256 + col + 8] if early else ps[2][:, col:col + 8]
                    p.op("pe", "matmul", gdst, A[hl][0:64, t * 128:(t + 1) * 128],
                         ksum_bf[:, hl, :], start=True, stop=True,
                         reads=[("A", hl, t // 4), ("ksum_bf", hl)],
                         writes=[("psb", 0) if early else ("ps", 2)], **MMK)
                gsrc = PF[:, 256:384] if early else ps[2][:, 0:128]
                gview = gsrc.rearrange("p (t n) -> p t n", n=8)
                p.op("dve", "tensor_tensor", out=gm[:, hl], in0=gview, in1=gbias[:], op=ALU.add,
                     reads=[("psb", 0) if early else ("ps", 2), "gbias"], writes=[("gm", hl)])
                yield
                for t in range(NT):
                    p.op("dve", "max", out=mx[:, t, :], in_=gm[:, hl, t, :], reads=[("gm", hl)],
                         writes=[("mx8",)])
                p.op("dve", "tensor_tensor", out=sel[:], in0=gm[:, hl],
                     in1=mx[:, :, 2:3].to_broadcast([128, NT, 8]), op=ALU.is_ge,
                     reads=[("gm", hl), ("mx8",)], writes=[("sel",)])
                p.op("dve", "scalar_tensor_tensor", out=Rq[:, :, hl, 64:72], in0=sel[:], scalar=-1.0,
                     in1=mtile[:], op0=ALU.add, op1=ALU.mult,
                     reads=[("sel",), "mtile"], writes=[("Rq_nm", hl)])
                yield
                for half in range(2):
                    nb = 1 if early else half
                    for tt in range(8):
                        t = half * 8 + tt
                        p.op("pe", "transpose", out=psb[nb][0:72, tt * 128:(tt + 1) * 128],
                             in_=Rq[:, t, hl, 0:72], identity=ident[:],
                             reads=[("Rq_rope", t, hl), ("Rq_rest", t, hl), ("Rq_nm", hl), "ident"],
                             writes=[("psb", nb)])
                    if half == 0:
                        p.op("act", "activation", out=A[hl][64:72, half * 1024:(half + 1) * 1024],
                             in_=psb[nb][64:72, :], func=AF.Copy, reads=[("psb", nb)],
                             writes=[("A", hl, 2 * half), ("A", hl, 2 * half + 1)])
                    else:
                        p.op("dve", "tensor_copy", out=A[hl][64:72, half * 1024:(half + 1) * 1024],
                             in_=psb[nb][64:72, :], reads=[("psb", nb)],
                             writes=[("A", hl, 2 * half), ("A", hl, 2 * half + 1)])
                    yield

            def mb_head(u):
                pr, hl = u // 2, u % 2
                return dict(
                    causal=True,
                    q_ap=lambda G, c0: A[hl][0:96, G * 512 + c0:(G + 1) * 512],
                    q_res=lambda G: ("A", hl, G),
                    k_ap=lambda kt: B[hl][0:96, kt * 128:(kt + 1) * 128],
                    k_res=lambda kt: ("B", hl, kt // 4),
                    v_ap=lambda kt: vaug[:, kt, hl, 0:65],
                    v_res=lambda kt: ("vh", kt, hl),
                    nkeys_tiles=None, dmask=dmask_mb, dmask_res="dmask_mb",
                    c_off=384 + u * 64, hl=hl, exp_scale=0.125,
                    sgt=sg, sg_res=lambda t: ("sgh", t, hl))

            nsb = 3 if stop not in ("sb1", "sbproj") else 1
            moba_early = False
            slot = slot_next
            slot_next = load_w(wrow(1))
            for _ in sb_proj_gen(0, slot):
                pass
            for pr in range(nsb):
                side = None
                if pr + 1 < nsb:
                    slot = slot_next
                    side = sb_proj_gen(pr + 1, slot)
                elif nsb == 3 and stop is None:
                    p.alias_transfer([("v", 0, t) for t in range(NT)],
                                     [("vh", t, h) for t in range(NT) for h in range(2)])
                    p.alias_transfer([("sg", 0, t) for t in range(NT)],
                                     [("sgh", t, h) for t in range(NT) for h in range(2)])
                    moba_setup(0)
                    mbs["slot_next"] = slot_next
                    side = mb_proj_gen(0, early=True)
                    moba_early = True
                if stop == "sbproj":
                    break
                for _ in sb_attn_gen(pr, side):
                    pass
                if pr + 1 < 3:
                    slot_next = load_w(wrow(pr + 2))
            p.alias_transfer([("v", 1, t) for t in range(NT)], [("hb", 0), ("hb", 1)])
            p.alias_transfer([("sg", 1, t) for t in range(NT)], [("mT", 0), ("mT", 1)])
            if stop in ("sb", "sb1", "sbproj"):
                break
            SGM = [sg2, sg]
            SGN = [1, 0]
            memst = {}

            def mem_pre_gen():
                slot_q = [mbs["slot_next"], None]
                memst["slot_q"] = slot_q
                slot_kv = load_w(wkv_d[l * 128:(l + 1) * 128, :])
                p.dma("gA", gA[:], mg_d[l:l + 1, :].partition_broadcast(128), writes=["gA"])
                for mt in range(2):
                    p.dma("memraw", ework[:], mem_d[mt * 128:(mt + 1) * 128, :], writes=[("e", 0), ("e", 1)])
                    norm_transpose(lambda t: ework[:], lambda t: [("e", 0), ("e", 1)], 1, mT, lambda t: ("mT", t),
                                   dst_t0=mt)
                    yield
                for pp in range(2):
                    bank = 2 + pp
                    for c in range(8):
                        p.op("pe", "matmul", ps[bank][:, 0:256], wbuf[:, slot_kv, c, pp * 128:(pp + 1) * 128],
                             mT[:, c, :], start=(c == 0), stop=(c == 7),
                             reads=[("w", slot_kv), ("mT", 0), ("mT", 1)], writes=[("ps", bank)], **MMK)
                    p.op("dve", "tensor_copy", out=mkT[:, pp, :], in_=ps[bank][:, 0:256], reads=[("ps", bank)],
                         writes=[("mkT", pp)])
                    yield
                for mt in range(2):
                    bank = 2 + mt
                    for c in range(8):
                        p.op("pe", "matmul", ps[bank][:, 0:256], mT[:, c, mt * 128:(mt + 1) * 128],
                             wbuf[:, slot_kv, c, 256:512], start=(c == 0), stop=(c == 7),
                             reads=[("w", slot_kv), ("mT", mt)], writes=[("ps", bank)], **MMK)
                    p.op("dve", "tensor_copy", out=mvaug[:, mt, :, 0:64],
                         in_=ps[bank][:, 0:256].rearrange("p (h d) -> p h d", d=64), reads=[("ps", bank)],
                         writes=[("mv", mt)])
                    yield
                slot_q[1] = load_w(wrow(7), 256, slot=slot_kv)
                p.alias_transfer([("mT", 0), ("mT", 1)], [("sg", 1, t) for t in range(NT)])
                yield from mem_proj_gen(0)

            def mem_proj_gen(pp):
                slot = memst["slot_q"][pp]
                for tg in range(4):
                    bank = 2 + tg % 2
                    for c in range(8):
                        p.op("pe", "matmul", ps[bank][:, :], wbuf[:, slot, c, 0:128],
                             hT[:, c, tg * 512:(tg + 1) * 512], start=(c == 0), stop=(c == 7),
                             reads=[("w", slot)] + [("hT", t) for t in range(4 * tg, 4 * tg + 4)],
                             writes=[("ps", bank)], **MMK)
                        if c == 3:
                            yield
                    p.op("act", "activation", out=A[pp][:, tg * 512:(tg + 1) * 512], in_=ps[bank][:, :],
                         func=AF.Copy, reads=[("ps", bank)], writes=[("A", pp, tg)])
                    yield
                for t in range(NT):
                    bank = 2 + t % 2
                    for c in range(8):
                        p.op("pe", "matmul", ps[bank][:, 0:128], hT[:, c, t * 128:(t + 1) * 128],
                             wbuf[:, slot, c, 128:256], start=(c == 0), stop=(c == 7),
                             reads=[("w", slot), ("hT", t)], writes=[("ps", bank)], **MMK)
                    p.op("dve", "tensor_copy", out=SGM[pp][:, t, :], in_=ps[bank][:, 0:128],
                         reads=[("ps", bank)], writes=[("sg", SGN[pp], t)])
                    yield
                sgf = MTS[:] if pp == 0 else sg[:].rearrange("p t n -> p (t n)")
                p.op("act", "activation", out=sgf, in_=sgf, func=AF.Silu,
                     reads=[("sg", SGN[pp], t) for t in range(NT)], writes=[("sg", SGN[pp], t) for t in range(NT)])
                yield

            def mem_heads(pp):
                heads = []
                for hl in range(2):
                    r0, r1 = hl * 64, (hl + 1) * 64
                    heads.append(dict(
                        causal=False,
                        q_ap=lambda G, c0, r0=r0, r1=r1: A[pp][r0:r1, G * 512 + c0:(G + 1) * 512],
                        q_res=lambda G: ("A", pp, G),
                        k_ap=lambda kt, r0=r0, r1=r1: mkT[r0:r1, pp, kt * 128:(kt + 1) * 128],
                        k_res=lambda kt: ("mkT", pp),
                        v_ap=lambda kt, hl=hl: mvaug[:, kt, 2 * pp + hl, 0:65],
                        v_res=lambda kt: ("mv", kt),
                        nkeys_tiles=2, dmask=None, dmask_res=None,
                        c_off=768 + (2 * pp + hl) * 64, hl=hl, exp_scale=0.125,
                        sgt=SGM[pp], sg_res=lambda t: ("sg", SGN[pp], t)))
                return heads

            if moba_early:
                moba_setup(1)
            else:
                p.alias_transfer([("v", 0, t) for t in range(NT)], [("vh", t, h) for t in range(NT) for h in range(2)])
                p.alias_transfer([("sg", 0, t) for t in range(NT)], [("sgh", t, h) for t in range(NT) for h in range(2)])
                moba_setup(0)
                moba_setup(1)
                mbs["slot_next"] = slot_next
                for _ in mb_proj_gen(0):
                    pass
            for u in range(6):
                side = mb_proj_gen(u + 1) if u + 1 < 6 else (mem_pre_gen() if stop is None else None)
                for _ in attn_stream_gen("sm", [mb_head(u)], side=side, side_every=int(os.environ.get("KMBE", "1")),
                                         side_burst=int(os.environ.get("KMBB", "1"))):
                    pass
            slot_next = mbs["slot_next"]
            p.alias_transfer([("sgh", t, h) for t in range(NT) for h in range(2)], [("sg", 0, t) for t in range(NT)])
            p.alias_transfer([("vh", t, h) for t in range(NT) for h in range(2)], [("v", 0, t) for t in range(NT)])
            if stop == "moba":
                break
            if stop is not None:
                for _ in mem_pre_gen():
                    pass
            for _ in attn_stream_gen("sm", mem_heads(0), side=mem_proj_gen(1), side_every=1, side_burst=4):
                pass
            for _ in attn_stream_gen("sm", mem_heads(1)):
                pass
            p.alias_transfer([("sg", 1, t) for t in range(NT)], [("mT", 0), ("mT", 1)])

            if "mixed" in dbg and li == int(os.environ.get("DBGLAYER", "0")):
                dumped.add("mixed")
                for t0 in range(0, NT, 4):
                    p.dma("dbg", dbg_d["mixed"][t0 * 128:(t0 + 4) * 128, :].rearrange("(t p) d -> p t d", p=128),
                          mixed[:, t0:t0 + 4, :], reads=[("mx", t) for t in range(t0, t0 + 4)])

            if stop == "mem":
                break
            wout_v = wbuf[:].rearrange("p s c n -> p (s c n)").rearrange("p (c n) -> p c n", n=D)
            wq["n"] = 0
            for sl in range(2):
                p.dma("w%d" % sl, wbuf[:, sl].rearrange("p c n -> p (c n)"),
                      wout_d[l * 128:(l + 1) * 128, sl * 4096:(sl + 1) * 4096], eng="pool", writes=[("w", sl)])
            dbg_x1 = "x1" in dbg and li == int(os.environ.get("DBGLAYER", "0"))
            if li + 1 < len(layers) and not dbg_x1:
                nmode = "layer"
                p.dma("gA", gA[:], ng_d[layers[li + 1]:layers[li + 1] + 1, :].partition_broadcast(128), writes=["gA"])
            elif li + 1 == len(layers) and do_final and stop is None and not dbg_x1:
                nmode = "final"
                p.dma("gA", gA[:], fg_d[0:1, :].partition_broadcast(128), writes=["gA"])
            else:
                nmode = "none"
            p.alias_transfer(["lnv", "rstd"], [("lnv_t", t) for t in range(NT)] + [("rstd_t", t) for t in range(NT)])
            junkb = ework[:].bitcast(BF16)[:, 0:D]

            def stT(t):
                for c in range(8):
                    p.op("pe", "transpose", out=psb[0][:, c * 128:(c + 1) * 128], in_=mixed[:, t, c * 128:(c + 1) * 128],
                         identity=ident[:], reads=[("mx", t), "ident"], writes=[("psb", 0)])
                if t % 2 == 0:
                    p.op("act", "activation", out=hT[:, :, t * 128:(t + 1) * 128],
                         in_=psb[0][:].rearrange("p (c n) -> p c n", n=128), func=AF.Copy,
                         reads=[("psb", 0)], writes=[("hT", t)])
                else:
                    p.op("dve", "tensor_copy", out=hT[:, :, t * 128:(t + 1) * 128],
                         in_=psb[0][:].rearrange("p (c n) -> p c n", n=128),
                         reads=[("psb", 0)], writes=[("hT", t)])

            def stM(t):
                for half in range(2):
                    bank = (2 * t + half) % 4
                    for c in range(8):
                        p.op("pe", "matmul", ps[bank][:, :], hT[:, c, t * 128:(t + 1) * 128],
                             wout_v[:, c, half * 512:(half + 1) * 512], start=(c == 0), stop=(c == 7),
                             reads=[("w", c // 4), ("hT", t)], writes=[("ps", bank)], **MMK)
                    p.op("dve", "tensor_tensor", out=xres[:, t, half * 512:(half + 1) * 512], in0=ps[bank][:, :],
                         in1=xres[:, t, half * 512:(half + 1) * 512], op=ALU.add,
                         reads=[("ps", bank), ("x", t)], writes=[("x", t)])
                if nmode != "none":
                    p.op("act", "activation", out=junkb, in_=xres[:, t, :], func=AF.Square, accum_out=ssq[:, t:t + 1],
                         reads=[("x", t)], writes=[("ssq", t), ("e", 0), ("e", 1)])
                    p.op("act", "activation", out=lnv[:, t:t + 1], in_=ssq[:, t:t + 1], func=AF.Ln, scale=1.0 / D,
                         bias=epsb[:], reads=[("ssq", t), "epsb"], writes=[("lnv_t", t)])
                    p.op("act", "activation", out=rstd[:, t:t + 1], in_=lnv[:, t:t + 1], func=AF.Exp, scale=-0.5,
                         reads=[("lnv_t", t)], writes=[("rstd_t", t)])

            def stN_dve(t):
                if nmode == "layer":
                    i = t % 2
                    p.op("dve", "scalar_tensor_tensor", out=hb[i][:], in0=xres[:, t, :], scalar=rstd[:, t:t + 1],
                         in1=gA[:], op0=ALU.mult, op1=ALU.mult,
                         reads=[("x", t), ("rstd_t", t), "gA"], writes=[("hb", i)])
                elif nmode == "final":
                    p.op("dve", "scalar_tensor_tensor", out=xres[:, t, :], in0=xres[:, t, :], scalar=rstd[:, t:t + 1],
                         in1=gA[:], op0=ALU.mult, op1=ALU.mult, reads=[("x", t), ("rstd_t", t), "gA"],
                         writes=[("x", t)])

            def stN(t):
                if nmode == "layer":
                    i = t % 2
                    for c in range(8):
                        p.op("pe", "transpose", out=psb[1][:, c * 128:(c + 1) * 128], in_=hb[i][:, c * 128:(c + 1) * 128],
                             identity=ident[:], reads=[("hb", i), "ident"], writes=[("psb", 1)])
                    p.op("act", "activation", out=hT[:, :, t * 128:(t + 1) * 128],
                         in_=psb[1][:].rearrange("p (c n) -> p c n", n=128), func=AF.Copy,
                         reads=[("psb", 1)], writes=[("hT", t)])
                elif nmode == "final":
                    if t % 4 == 3:
                        t0 = t - 3
                        dst = out_d[t0 * 128:(t0 + 4) * 128, :].rearrange("(t p) d -> p t d", p=128)
                        p.dma("out", dst, xres[:, t0:t0 + 4, :], reads=[("x", tt) for tt in range(t0, t0 + 4)])

            for it in range(NT + 3):
                if 0 <= it - 3 < NT:
                    stN_dve(it - 3)
                if it < NT:
                    stT(it)
                if 0 <= it - 1 < NT:
                    stM(it - 1)
                if 0 <= it - 3 < NT:
                    stN(it - 3)
            p.alias_transfer([("lnv_t", t) for t in range(NT)] + [("rstd_t", t) for t in range(NT)], ["lnv", "rstd"])
            if nmode == "layer":
                fused_norm_done = True
            if nmode == "final":
                final_done = True
            if "x1" in dbg and li == int(os.environ.get("DBGLAYER", "0")):
                for t0 in range(0, NT, 4):
                    p.dma("dbg", dbg_d["x1"][t0 * 128:(t0 + 4) * 128, :].rearrange("(t p) d -> p t d", p=128),
                          xres[:, t0:t0 + 4, :], reads=[("x", t) for t in range(t0, t0 + 4)])

        if "mixed" in dbg and "mixed" not in dumped:
            for t0 in range(0, NT, 4):
                p.dma("dbg", dbg_d["mixed"][t0 * 128:(t0 + 4) * 128, :].rearrange("(t p) d -> p t d", p=128),
                      mixed[:, t0:t0 + 4, :], reads=[("mx", t) for t in range(t0, t0 + 4)])
        if final_done:
            pass
        elif do_final and (stop in (None, "normfinal") or "FINAL" in os.environ):
            p.dma("gA", gA[:], fg_d[0:1, :].partition_broadcast(128), writes=["gA"])
            rmsnorm_stats(lambda t: xres[:, t, :], lambda t: [("x", t)], NT)
            for t in range(NT):
                p.op("dve", "scalar_tensor_tensor", out=xres[:, t, :], in0=xres[:, t, :], scalar=rstd[:, t:t + 1],
                     in1=gA[:], op0=ALU.mult, op1=ALU.mult, reads=[("x", t), "rstd", "gA"], writes=[("x", t)])
        if not final_done:
            for t0 in range(0, NT, 4):
                dst = out_d[t0 * 128:(t0 + 4) * 128, :].rearrange("(t p) d -> p t d", p=128)
                p.dma("out", dst, xres[:, t0:t0 + 4, :], reads=[("x", t) for t in range(t0, t0 + 4)])
        p.wait_all_dma("out")
        if "out2" in dbg:
            o2 = dram("dbg_out2", [T, D], F32, kind="ExternalOutput")
            for t0 in range(0, NT, 4):
                p.dma("dbg2", o2[t0 * 128:(t0 + 4) * 128, :].rearrange("(t p) d -> p t d", p=128),
                      xres[:, t0:t0 + 4, :], reads=[("x", t) for t in range(NT)] + ["rstd"])
            p.wait_all_dma("dbg2")
        if "dbg" in p.chan_n:
            p.wait_all_dma("dbg")
        p.emit()
        build_program.stats = p.stats
    return nc


def host_constants():
    bf = ml_dtypes.bfloat16
    c = {}
    c["c_ident"] = np.eye(128, dtype=np.float32).astype(bf)
    j = np.arange(128)[:, None]
    s = np.arange(128)[None, :]
    c["c_negtri"] = np.where(j >= s, -1.0, 0.0).astype(np.float32).astype(bf)
    c["c_negones"] = np.full((128, 128), -1.0, np.float32).astype(bf)
    c["c_dmask_sb"] = np.where(j >= s, NEG, 0.0).astype(np.float32).astype(bf)
    c["c_dmask_mb"] = np.where(j > s, NEG, 0.0).astype(np.float32).astype(bf)
    kind = np.zeros((8, T), np.float32)
    for m in range(8):
        kind[m, m * 256:(m + 1) * 256] = 1.0
    c["c_kind"] = kind.astype(bf)
    pos = (np.arange(NT)[None, :] * 128 + np.arange(128)[:, None]).astype(np.float64)
    inv = 500000.0 ** (-np.arange(8, dtype=np.float64) * 2.0 / 16.0)
    ang = pos[:, :, None] * inv[None, None, :]
    cc = np.concatenate([np.cos(ang), np.cos(ang)], axis=2)
    ss = np.concatenate([np.sin(ang), np.sin(ang)], axis=2)
    c["c_cc"] = cc.reshape(128, NT * 16).astype(np.float32)
    c["c_ss"] = ss.reshape(128, NT * 16).astype(np.float32)
    own = np.arange(NT) // 2
    nb = np.arange(8)
    gb = np.where(nb[None, :] < own[:, None], 0.0, -1e30).astype(np.float32)
    mt = np.where(nb[None, :] < own[:, None], -NEG, 0.0).astype(np.float32)
    c["c_gbias"] = np.broadcast_to(gb[None], (128, NT, 8)).reshape(128, NT * 8).copy()
    c["c_mtile"] = np.broadcast_to(mt[None], (128, NT, 8)).reshape(128, NT * 8).copy()
    return c


def relayout_w_in(w_in):
    Lw = w_in.shape[0]
    out = np.zeros((Lw, 8, D, 512), np.float32)
    for pr in range(3):
        for k in range(4):
            out[:, pr, :, k * 128:(k + 1) * 128] = w_in[:, :, k * 384 + pr * 128: k * 384 + (pr + 1) * 128]
            out[:, 3 + pr, :, k * 128:(k + 1) * 128] = w_in[:, :, 1536 + k * 384 + pr * 128: 1536 + k * 384 + (pr + 1) * 128]
    for pp in range(2):
        out[:, 6 + pp, :, 0:128] = w_in[:, :, 3072 + pp * 128:3072 + (pp + 1) * 128]
        out[:, 6 + pp, :, 128:256] = w_in[:, :, 3328 + pp * 128:3328 + (pp + 1) * 128]
    out = out.reshape(Lw, 8, 8, 128, 512).transpose(0, 1, 3, 2, 4)
    return np.ascontiguousarray(out).reshape(Lw * 8 * 128, 8 * 512)


def pmajor(w):
    Lw, _, n = w.shape
    return np.ascontiguousarray(w.reshape(Lw, 8, 128, n).transpose(0, 2, 1, 3)).reshape(Lw * 128, 8 * n)


_CACHE = {}


def kernel(x, mem, norm_g, w_in, mem_norm_g, w_mem_kv, w_out, final_norm_g):
    x = np.asarray(x, np.float32)
    mem = np.asarray(mem, np.float32)
    consts = host_constants()
    shared = dict(consts)
    shared["w_in_r"] = relayout_w_in(np.asarray(w_in, np.float32))
    shared["w_mem_kv"] = pmajor(np.asarray(w_mem_kv, np.float32))
    shared["w_out"] = pmajor(np.asarray(w_out, np.float32))
    shared["norm_g"] = np.ascontiguousarray(np.asarray(norm_g, np.float32))
    shared["mem_norm_g"] = np.ascontiguousarray(np.asarray(mem_norm_g, np.float32))
    shared["final_norm_g"] = np.ascontiguousarray(np.asarray(final_norm_g, np.float32).reshape(1, D))
    if "nc" not in _CACHE:
        _CACHE["nc"] = build_program()
    nc = _CACHE["nc"]
    in_maps = []
    for b in range(8):
        m = dict(shared)
        m["x"] = np.ascontiguousarray(x[b])
        m["mem"] = np.ascontiguousarray(mem[b])
        in_maps.append(m)
    res = run_bass_kernel_spmd(nc, in_maps, core_ids=list(range(8)))
    return np.stack([np.asarray(r["out"], np.float32) for r in res.results], axis=0)
```
